# Optimizing a Trainium2 kernel written in Bass

```python
import math
import jax, jax.numpy as jnp
from jax import lax
import numpy as np

D_MODEL = 1024
BATCH = 2
SEQ = 16384
DEPTH = 2

HEAD_DIM = 64
BLOCK = 128
EPS = 1e-6
NEG_INF = -1e30
SWA_HEADS = 8
SWA_KV_HEADS = 2
SWA_WINDOW = 128
MLA_HEADS = 8
MLA_Q_RANK = 256
MLA_KV_RANK = 128
MLA_NOPE_DIM = 64
MLA_ROPE_DIM = 32
MLA_V_DIM = 64
MLA_QK_DIM = MLA_NOPE_DIM + MLA_ROPE_DIM
ROPE_THETA = 10000.0
DIFF_HEADS = 8
DIFF_DIM = 64
MEM_TOKENS = 256
MEM_HEADS = 4
MEM_HEAD_DIM = 128
D_FF = -(-8 * D_MODEL // (3 * 256)) * 256

SWA_Q_W = SWA_HEADS * HEAD_DIM
SWA_KV_W = SWA_KV_HEADS * HEAD_DIM
EVEN_SPLITS = (SWA_Q_W, SWA_Q_W + SWA_KV_W, SWA_Q_W + 2 * SWA_KV_W,
               SWA_Q_W + 2 * SWA_KV_W + MLA_Q_RANK,
               SWA_Q_W + 2 * SWA_KV_W + MLA_Q_RANK + MLA_KV_RANK)
IN_EVEN = SWA_Q_W + 2 * SWA_KV_W + MLA_Q_RANK + MLA_KV_RANK + MLA_ROPE_DIM
MIX_EVEN = SWA_HEADS * HEAD_DIM + MLA_HEADS * MLA_V_DIM
DIFF_W = DIFF_HEADS * 2 * DIFF_DIM
N_EVEN = (DEPTH + 1) // 2
N_ODD = DEPTH // 2

kernel_name = "hybrid_swa_mla_diffattn_mem_swiglu"


def rms_norm(x, g):
    xf = x.astype(jnp.float32)
    y = xf * lax.rsqrt(jnp.mean(xf * xf, axis=-1, keepdims=True) + EPS)
    return (y * g.astype(jnp.float32)).astype(x.dtype)


def alibi_slopes(n):
    return jnp.asarray([2.0 ** (-8.0 * (i + 1) / n) for i in range(n)], jnp.float32)


def rope(x, positions):
    half = x.shape[-1] // 2
    inv = ROPE_THETA ** (-jnp.arange(half, dtype=jnp.float32) / half)
    ang = positions.astype(jnp.float32)[:, :, None, None] * inv
    cos, sin = jnp.cos(ang), jnp.sin(ang)
    xf = x.astype(jnp.float32)
    x1, x2 = xf[..., :half], xf[..., half:]
    return jnp.concatenate([x1 * cos - x2 * sin, x2 * cos + x1 * sin], axis=-1).astype(x.dtype)


def _query_blocks(q):
    B, S = q.shape[:2]
    return q.reshape(B, S // BLOCK, BLOCK, *q.shape[2:]).swapaxes(0, 1)


def _merge_blocks(o):
    nb, B = o.shape[:2]
    return o.swapaxes(0, 1).reshape(B, nb * BLOCK, *o.shape[3:])


def _causal_probs(qblk, k, n, slopes):
    S = k.shape[1]
    s = jnp.einsum('bqhd,bkhd->bhqk', qblk, k, preferred_element_type=jnp.float32) * (qblk.shape[-1] ** -0.5)
    dist = (n * BLOCK + jnp.arange(BLOCK))[:, None] - jnp.arange(S)[None, :]
    if slopes is not None:
        s = s - slopes[None, :, None, None] * dist.astype(jnp.float32)
    s = jnp.where(dist >= 0, s, NEG_INF)
    return jax.nn.softmax(s, axis=-1)


def swa_sink_attention(q, k, v, sinks, slopes):
    B, S, H, D = q.shape
    Hk = k.shape[2]
    G = H // Hk
    nb = S // BLOCK
    qb = q.reshape(B, nb, BLOCK, Hk, G, D)
    pad = jnp.zeros((B, BLOCK, Hk, D), k.dtype)
    kp = jnp.concatenate([pad, k], axis=1).reshape(B, nb + 1, BLOCK, Hk, D)
    vp = jnp.concatenate([pad.astype(v.dtype), v], axis=1).reshape(B, nb + 1, BLOCK, Hk, D)
    kb = jnp.concatenate([kp[:, :-1], kp[:, 1:]], axis=2)
    vb = jnp.concatenate([vp[:, :-1], vp[:, 1:]], axis=2)
    s = jnp.einsum('bnqhgd,bnkhd->bnhgqk', qb, kb, preferred_element_type=jnp.float32) * (D ** -0.5)
    dist = jnp.arange(BLOCK)[:, None] - jnp.arange(2 * BLOCK)[None, :] + BLOCK
    src = jnp.arange(nb)[:, None, None] * BLOCK - BLOCK + jnp.arange(2 * BLOCK)[None, None, :]
    valid = ((dist >= 0) & (dist < SWA_WINDOW))[None] & (src >= 0)
    s = s - slopes.reshape(Hk, G)[:, :, None, None] * dist.astype(jnp.float32)
    s = jnp.where(valid[None, :, None, None], s, NEG_INF)
    sink = jnp.broadcast_to(sinks.astype(jnp.float32).reshape(Hk, G)[None, None, :, :, None, None],
                            s.shape[:-1] + (1,))
    p = jax.nn.softmax(jnp.concatenate([s, sink], axis=-1), axis=-1)[..., :-1]
    o = jnp.einsum('bnhgqk,bnkhd->bnqhgd', p.astype(v.dtype), vb)
    return o.reshape(B, S, H * D)


def mla_attention(q, k, v):
    def body(args):
        qblk, n = args
        p = _causal_probs(qblk, k, n, None)
        return jnp.einsum('bhqk,bkhd->bqhd', p.astype(v.dtype), v)
    o = _merge_blocks(lax.map(body, (_query_blocks(q), jnp.arange(q.shape[1] // BLOCK))))
    return o.reshape(q.shape[0], q.shape[1], -1)


def diff_attention(q1, q2, k1, k2, v, lam, slopes):
    def body(args):
        q1b, q2b, n = args
        a = _causal_probs(q1b, k1, n, slopes) - lam * _causal_probs(q2b, k2, n, slopes)
        return jnp.einsum('bhqk,bkhd->bqhd', a.astype(v.dtype), v)
    return _merge_blocks(lax.map(body, (_query_blocks(q1), _query_blocks(q2), jnp.arange(q1.shape[1] // BLOCK))))


def even_mixer(h, positions, w_in, swa_q_gain, swa_k_gain, sinks, q_latent_norm, kv_latent_norm,
               w_uq, w_ukv, mla_q_gain, mla_k_gain, w_out):
    B, S, _ = h.shape
    z = h @ w_in
    qa, ka, va, cq, ckv, kr = jnp.split(z, EVEN_SPLITS, axis=-1)
    qa = rms_norm(qa.reshape(B, S, SWA_HEADS, HEAD_DIM), swa_q_gain)
    ka = rms_norm(ka.reshape(B, S, SWA_KV_HEADS, HEAD_DIM), swa_k_gain)
    va = va.reshape(B, S, SWA_KV_HEADS, HEAD_DIM)
    out_a = swa_sink_attention(qa, ka, va, sinks, alibi_slopes(SWA_HEADS))
    q_full = (rms_norm(cq, q_latent_norm) @ w_uq).reshape(B, S, MLA_HEADS, MLA_QK_DIM)
    kv = (rms_norm(ckv, kv_latent_norm) @ w_ukv).reshape(B, S, MLA_HEADS, MLA_NOPE_DIM + MLA_V_DIM)
    k_full = jnp.concatenate([kv[..., :MLA_NOPE_DIM],
                              jnp.broadcast_to(kr[:, :, None, :], (B, S, MLA_HEADS, MLA_ROPE_DIM))], axis=-1)
    vb = kv[..., MLA_NOPE_DIM:]
    q_full = rms_norm(q_full, mla_q_gain)
    k_full = rms_norm(k_full, mla_k_gain)
    qb = jnp.concatenate([q_full[..., :MLA_NOPE_DIM], rope(q_full[..., MLA_NOPE_DIM:], positions)], axis=-1)
    kb = jnp.concatenate([k_full[..., :MLA_NOPE_DIM], rope(k_full[..., MLA_NOPE_DIM:], positions)], axis=-1)
    out_b = mla_attention(qb, kb, vb)
    return jnp.concatenate([out_a, out_b], axis=-1) @ w_out


def odd_mixer(h, w_qkv, q_gain, k_gain, lambdas, subln, w_out, lambda_init):
    B, S, _ = h.shape
    q, k, v = jnp.split(h @ w_qkv, 3, axis=-1)
    q = rms_norm(q.reshape(B, S, DIFF_HEADS, 2, DIFF_DIM), q_gain)
    k = rms_norm(k.reshape(B, S, DIFF_HEADS, 2, DIFF_DIM), k_gain)
    v = v.reshape(B, S, DIFF_HEADS, 2 * DIFF_DIM)
    lf = lambdas.astype(jnp.float32)
    lam = jnp.exp(jnp.sum(lf[0] * lf[1])) - jnp.exp(jnp.sum(lf[2] * lf[3])) + lambda_init
    o = diff_attention(q[:, :, :, 0], q[:, :, :, 1], k[:, :, :, 0], k[:, :, :, 1], v, lam,
                       alibi_slopes(DIFF_HEADS))
    o = rms_norm(o, subln) * (1.0 - lambda_init)
    return o.reshape(B, S, DIFF_W) @ w_out


def mem_attention(h, m, w_q, w_kv, q_gain, k_gain, w_out):
    B, S, _ = h.shape
    q = rms_norm((h @ w_q).reshape(B, S, MEM_HEADS, MEM_HEAD_DIM), q_gain)
    k, v = jnp.split(m @ w_kv, 2, axis=-1)
    k = rms_norm(k.reshape(B, MEM_TOKENS, MEM_HEADS, MEM_HEAD_DIM), k_gain)
    v = v.reshape(B, MEM_TOKENS, MEM_HEADS, MEM_HEAD_DIM)
    s = jnp.einsum('bshd,bmhd->bhsm', q, k, preferred_element_type=jnp.float32) * (MEM_HEAD_DIM ** -0.5)
    p = jax.nn.softmax(s, axis=-1)
    o = jnp.einsum('bhsm,bmhd->bshd', p.astype(v.dtype), v)
    return o.reshape(B, S, MEM_HEADS * MEM_HEAD_DIM) @ w_out


def swiglu(h, w_gate, w_up, w_down):
    return (jax.nn.silu(h @ w_gate) * (h @ w_up)) @ w_down


def setup_inputs(seed: int = 0) -> dict:
    key = jax.random.key(seed)
    ks = iter(jax.random.split(key, 48))

    def w(shape, fan_in):
        return jax.random.normal(next(ks), shape, jnp.float32) * (fan_in ** -0.5)

    def gain(shape):
        return 1.0 + 0.02 * jax.random.normal(next(ks), shape, jnp.float32)

    x = jax.random.normal(next(ks), (BATCH, SEQ, D_MODEL), jnp.float32)
    mem = jax.random.normal(next(ks), (BATCH, MEM_TOKENS, D_MODEL), jnp.float32)
    offset = jax.random.randint(next(ks), (BATCH, 1), 0, 1024, dtype=jnp.int32)
    positions = (offset + jnp.arange(SEQ, dtype=jnp.int32)[None, :]).astype(jnp.int32)
    return {
        "x": x,
        "mem": mem,
        "positions": positions,
        "mix_norm": gain((DEPTH, D_MODEL)),
        "ev_w_in": w((N_EVEN, D_MODEL, IN_EVEN), D_MODEL),
        "ev_swa_q_gain": gain((N_EVEN, HEAD_DIM)),
        "ev_swa_k_gain": gain((N_EVEN, HEAD_DIM)),
        "ev_sinks": 0.5 * jax.random.normal(next(ks), (N_EVEN, SWA_HEADS), jnp.float32),
        "ev_q_latent_norm": gain((N_EVEN, MLA_Q_RANK)),
        "ev_kv_latent_norm": gain((N_EVEN, MLA_KV_RANK)),
        "ev_w_uq": w((N_EVEN, MLA_Q_RANK, MLA_HEADS * MLA_QK_DIM), MLA_Q_RANK),
        "ev_w_ukv": w((N_EVEN, MLA_KV_RANK, MLA_HEADS * (MLA_NOPE_DIM + MLA_V_DIM)), MLA_KV_RANK),
        "ev_mla_q_gain": gain((N_EVEN, MLA_QK_DIM)),
        "ev_mla_k_gain": gain((N_EVEN, MLA_QK_DIM)),
        "ev_w_out": w((N_EVEN, MIX_EVEN, D_MODEL), MIX_EVEN),
        "od_w_qkv": w((N_ODD, D_MODEL, 3 * DIFF_W), D_MODEL),
        "od_q_gain": gain((N_ODD, DIFF_DIM)),
        "od_k_gain": gain((N_ODD, DIFF_DIM)),
        "od_lambda": 0.1 * jax.random.normal(next(ks), (N_ODD, 4, DIFF_DIM), jnp.float32),
        "od_subln": gain((N_ODD, 2 * DIFF_DIM)),
        "od_w_out": w((N_ODD, DIFF_W, D_MODEL), DIFF_W),
        "mem_q_norm": gain((DEPTH, D_MODEL)),
        "mem_kv_norm": gain((DEPTH, D_MODEL)),
        "mem_w_q": w((DEPTH, D_MODEL, MEM_HEADS * MEM_HEAD_DIM), D_MODEL),
        "mem_w_kv": w((DEPTH, D_MODEL, 2 * MEM_HEADS * MEM_HEAD_DIM), D_MODEL),
        "mem_q_gain": gain((DEPTH, MEM_HEAD_DIM)),
        "mem_k_gain": gain((DEPTH, MEM_HEAD_DIM)),
        "mem_w_out": w((DEPTH, MEM_HEADS * MEM_HEAD_DIM, D_MODEL), MEM_HEADS * MEM_HEAD_DIM),
        "ffn_norm": gain((DEPTH, D_MODEL)),
        "ffn_w_gate": w((DEPTH, D_MODEL, D_FF), D_MODEL),
        "ffn_w_up": w((DEPTH, D_MODEL, D_FF), D_MODEL),
        "ffn_w_down": w((DEPTH, D_FF, D_MODEL), D_FF),
    }


def reference(x, mem, positions, mix_norm, ev_w_in, ev_swa_q_gain, ev_swa_k_gain, ev_sinks,
              ev_q_latent_norm, ev_kv_latent_norm, ev_w_uq, ev_w_ukv, ev_mla_q_gain, ev_mla_k_gain,
              ev_w_out, od_w_qkv, od_q_gain, od_k_gain, od_lambda, od_subln, od_w_out,
              mem_q_norm, mem_kv_norm, mem_w_q, mem_w_kv, mem_q_gain, mem_k_gain, mem_w_out,
              ffn_norm, ffn_w_gate, ffn_w_up, ffn_w_down):
    for l in range(DEPTH):
        h = rms_norm(x, mix_norm[l])
        if l % 2 == 0:
            e = l // 2
            y = even_mixer(h, positions, ev_w_in[e], ev_swa_q_gain[e], ev_swa_k_gain[e], ev_sinks[e],
                           ev_q_latent_norm[e], ev_kv_latent_norm[e], ev_w_uq[e], ev_w_ukv[e],
                           ev_mla_q_gain[e], ev_mla_k_gain[e], ev_w_out[e])
        else:
            o = l // 2
            lambda_init = 0.8 - 0.6 * math.exp(-0.3 * l)
            y = odd_mixer(h, od_w_qkv[o], od_q_gain[o], od_k_gain[o], od_lambda[o], od_subln[o],
                          od_w_out[o], lambda_init)
        x = x + y
        x = x + mem_attention(rms_norm(x, mem_q_norm[l]), rms_norm(mem, mem_kv_norm[l]), mem_w_q[l],
                              mem_w_kv[l], mem_q_gain[l], mem_k_gain[l], mem_w_out[l])
        x = x + swiglu(rms_norm(x, ffn_norm[l]), ffn_w_gate[l], ffn_w_up[l], ffn_w_down[l])
    return x
```

```python
import math
import types
import numpy as np
import ml_dtypes
import concourse.bass as bass
import concourse.mybir as mybir
from concourse.bass_utils import run_bass_kernel_spmd

F32 = mybir.dt.float32
BF16 = mybir.dt.bfloat16
I32 = mybir.dt.int32
AF = mybir.ActivationFunctionType
ALU = mybir.AluOpType
AX = mybir.AxisListType
bf = ml_dtypes.bfloat16

SAME_ENGINE_SYNC = True
EPS = 1e-6
D = 1024
DFF = 2816
NFC = 22
MEMT = 256


class Buf:
    __slots__ = ("name", "lw", "rd", "sem", "cnt")

    def __init__(self, name):
        self.name = name
        self.lw = None
        self.rd = []
        self.sem = None
        self.cnt = 0


class Op:
    __slots__ = ("eng", "fn", "deps", "dmabuf", "dmaval", "need_sig", "sigval", "dmawaits", "idx")


class Rec:
    ENGS = ("pe", "act", "dve", "pool", "sp")

    def __init__(self, nc):
        self.nc = nc
        self.ops = {e: [] for e in self.ENGS}
        self.n = 0

    def op(self, eng, fn, reads=(), writes=(), dma=None):
        o = Op()
        o.eng = eng
        if fn is not None and fn.__closure__:
            cells = []
            for c in fn.__closure__:
                try:
                    cells.append(types.CellType(c.cell_contents))
                except ValueError:
                    cells.append(c)
            f2 = types.FunctionType(fn.__code__, fn.__globals__, fn.__name__, fn.__defaults__, tuple(cells))
            f2.__kwdefaults__ = fn.__kwdefaults__
            fn = f2
        o.fn = fn
        o.dmabuf = dma
        o.need_sig = False
        o.sigval = None
        o.idx = self.n
        self.n += 1
        deps = {}
        for b in reads:
            if b.lw is not None:
                deps[id(b.lw)] = b.lw
        for b in writes:
            if b.lw is not None:
                deps[id(b.lw)] = b.lw
            for r in b.rd:
                deps[id(r)] = r
        o.deps = []
        o.dmawaits = []
        for d in deps.values():
            if d.dmabuf is not None:
                o.dmawaits.append((d.dmabuf, d.dmabuf.cnt * 16))
            else:
                if d.eng == eng and (eng == "pe" or not SAME_ENGINE_SYNC):
                    continue
                d.need_sig = True
                o.deps.append(d)
        if dma is not None:
            if dma.sem is None:
                dma.sem = self.nc.alloc_semaphore("ds_%d" % self.n)
            dma.cnt += 1
            o.dmaval = dma.cnt * 16
            assert o.dmaval < 65000, ("dma sem overflow", dma.name)
        for b in reads:
            b.rd.append(o)
        for b in writes:
            b.lw = o
            b.rd = []
        self.ops[eng].append(o)
        return o

    def emit(self, final_bufs=()):
        nc = self.nc
        esem = {e: nc.alloc_semaphore("es_" + e) for e in ("pe", "act", "dve", "pool")}
        for e in ("pe", "act", "dve", "pool"):
            c = 0
            for o in self.ops[e]:
                if o.need_sig and o.dmabuf is None:
                    c += 1
                    o.sigval = c
            assert c < 65000, ("engine sem overflow", e, c)
        fin = Op()
        fin.eng = "sp"
        fin.fn = None
        fin.deps = []
        fin.dmabuf = None
        fin.need_sig = False
        seen_b = {}
        for b in final_bufs:
            if b.sem is not None:
                seen_b[id(b)] = (b, b.cnt * 16)
        fin.dmawaits = list(seen_b.values())
        self.ops["sp"].append(fin)

        def run(ename, eng):
            seen = {}
            for o in self.ops[ename]:
                for d in o.deps:
                    key = ("e", d.eng)
                    if seen.get(key, 0) < d.sigval:
                        eng.wait_ge(esem[d.eng], d.sigval)
                        seen[key] = d.sigval
                for (b, v) in o.dmawaits:
                    key = ("d", id(b))
                    if seen.get(key, 0) < v:
                        eng.wait_ge(b.sem, v)
                        seen[key] = v
                if o.fn is None:
                    continue
                ins = o.fn(eng)
                if o.dmabuf is not None:
                    ins.then_inc(o.dmabuf.sem, 16)
                elif o.need_sig:
                    ins.then_inc(esem[ename], 1)

        with nc.Block() as block:
            @block.tensor
            def _(e):
                run("pe", e)

            @block.scalar
            def _(e):
                run("act", e)

            @block.vector
            def _(e):
                run("dve", e)

            @block.gpsimd
            def _(e):
                run("pool", e)

            @block.sync
            def _(e):
                run("sp", e)


class Tile:
    def __init__(self, nc, name, shape, dtype, psum=False):
        if psum:
            self.t = nc.alloc_psum_tensor(name, list(shape), dtype)
        else:
            self.t = nc.alloc_sbuf_tensor(name, list(shape), dtype)
        self.b = Buf(name)
        self.shape = tuple(shape)

    def __getitem__(self, k):
        return self.t[k]


def gtile(j, lt):
    p = lt // 2
    return 8 * p + (j if lt % 2 == 0 else 7 - j)


def owner(g):
    p, o = g // 8, g % 8
    return (o, 2 * p) if o < 4 else (7 - o, 2 * p + 1)


SLOPES = [2.0 ** (-(i + 1)) for i in range(8)]
LAMBDA_INIT = 0.8 - 0.6 * math.exp(-0.3 * 1)

W_SPECS = [
    ("w_in", 8 * 1184), ("w_uq", 2 * 768), ("w_ukv", 1024),
    ("ev_wo", 8 * 1024), ("mq0", 8 * 512), ("mkv0", 8 * 1024), ("mo0", 4 * 1024),
    ("gu0", NFC * 2 * 8 * 128), ("wd0", NFC * 1024),
    ("qkv", 8 * 3072),
    ("od_wo", 8 * 1024), ("mq1", 8 * 512), ("mkv1", 8 * 1024), ("mo1", 4 * 1024),
    ("gu1", NFC * 2 * 8 * 128), ("wd1", NFC * 1024),
]
W_OFF = {}
_o = 0
for _n, _s in W_SPECS:
    W_OFF[_n] = (_o, _s)
    _o += _s
W_TOTAL = _o
STAGE_W = {1: ["w_in", "w_uq", "w_ukv"],
           2: ["mkv0", "ev_wo", "mq0", "mo0", "gu0", "wd0", "qkv"],
           3: ["mkv1", "od_wo", "mq1", "mo1", "gu1", "wd1"]}
CH = 1024

GROW = [("swa_qg", 64), ("swa_kg", 64), ("qlat", 256), ("kvlat", 128), ("mla_qg", 96), ("mla_kg", 96),
        ("od_qg", 64), ("od_kg", 64), ("subln", 128), ("memk0", 128), ("memk1", 128),
        ("sinks", 8), ("lam", 256), ("invf", 16)]
GROW_OFF = {}
_o = 0
for _n, _s in GROW:
    GROW_OFF[_n] = (_o, _s)
    _o += _s
GROW_TOTAL = _o
GCOL = ["mix0", "mix1", "mqn0", "mqn1", "mkvn0", "mkvn1", "ffn0", "ffn1"]


def wrearr(w):
    K, N = w.shape
    return np.ascontiguousarray(w.reshape(K // 128, 128, N).transpose(1, 0, 2).reshape(128, -1))


def host_prep(inp):
    f = lambda a: np.asarray(a, dtype=np.float32)
    Wb = {}
    Wb["w_in"] = wrearr(f(inp["ev_w_in"])[0])
    wuq = f(inp["ev_w_uq"])[0].reshape(256, 8, 96)
    Wb["w_uq"] = wrearr(np.concatenate([wuq[:, :, :64].reshape(256, 512), wuq[:, :, 64:].reshape(256, 256)], axis=1))
    wukv = f(inp["ev_w_ukv"])[0].reshape(128, 8, 128)
    Wb["w_ukv"] = wrearr(np.concatenate([wukv[:, :, :64].reshape(128, 512), wukv[:, :, 64:].reshape(128, 512)], axis=1))
    Wb["ev_wo"] = wrearr(f(inp["ev_w_out"])[0])
    Wb["qkv"] = wrearr(f(inp["od_w_qkv"])[0])
    Wb["od_wo"] = wrearr(f(inp["od_w_out"])[0])
    for l in range(2):
        Wb["mq%d" % l] = wrearr(f(inp["mem_w_q"])[l])
        Wb["mkv%d" % l] = wrearr(f(inp["mem_w_kv"])[l])
        Wb["mo%d" % l] = wrearr(f(inp["mem_w_out"])[l])
        g = f(inp["ffn_w_gate"])[l].reshape(8, 128, NFC, 128)
        u = f(inp["ffn_w_up"])[l].reshape(8, 128, NFC, 128)
        gu = np.stack([g, u], axis=0)
        Wb["gu%d" % l] = np.ascontiguousarray(gu.transpose(2, 3, 0, 1, 4).reshape(128, -1))
        wd = f(inp["ffn_w_down"])[l].reshape(11, 2, 128, 2, 512)
        Wb["wd%d" % l] = np.ascontiguousarray(wd.transpose(2, 3, 0, 1, 4).reshape(128, -1))
    for n, s in W_SPECS:
        assert Wb[n].shape == (128, s), (n, Wb[n].shape, s)
    rows = {"swa_qg": f(inp["ev_swa_q_gain"])[0], "swa_kg": f(inp["ev_swa_k_gain"])[0],
            "qlat": f(inp["ev_q_latent_norm"])[0], "kvlat": f(inp["ev_kv_latent_norm"])[0],
            "mla_qg": f(inp["ev_mla_q_gain"])[0], "mla_kg": f(inp["ev_mla_k_gain"])[0],
            "od_qg": f(inp["od_q_gain"])[0], "od_kg": f(inp["od_k_gain"])[0], "subln": f(inp["od_subln"])[0],
            "memk0": f(inp["mem_k_gain"])[0], "memk1": f(inp["mem_k_gain"])[1],
            "sinks": f(inp["ev_sinks"])[0], "lam": f(inp["od_lambda"])[0].reshape(-1),
            "invf": np.asarray([10000.0 ** (-(i / 16.0)) for i in range(16)], np.float32)}
    grow = np.concatenate([rows[n] for n, _ in GROW]).astype(np.float32)
    cols = {"mix0": f(inp["mix_norm"])[0], "mix1": f(inp["mix_norm"])[1],
            "mqn0": f(inp["mem_q_norm"])[0], "mqn1": f(inp["mem_q_norm"])[1],
            "mkvn0": f(inp["mem_kv_norm"])[0], "mkvn1": f(inp["mem_kv_norm"])[1],
            "ffn0": f(inp["ffn_norm"])[0], "ffn1": f(inp["ffn_norm"])[1]}
    gcol = np.concatenate([cols[n].reshape(8, 128).T for n in GCOL], axis=1)
    mqg = np.stack([f(inp["mem_q_gain"])[0], f(inp["mem_q_gain"])[1]], axis=1)
    gcol = np.ascontiguousarray(np.concatenate([gcol, mqg], axis=1).astype(np.float32))
    return Wb, grow, gcol


def const_tables():
    c = {}
    c["ident"] = np.eye(128, dtype=np.float32).astype(bf)
    ik = np.arange(128)[:, None]
    iq = np.arange(128)[None, :]
    c["mcur"] = (iq >= ik).astype(np.float32).astype(bf)
    c["mprev"] = (iq < ik).astype(np.float32).astype(bf)
    qa = np.zeros((4, 8, 128), np.float32)
    for h in range(8):
        s = SLOPES[h]
        q = np.arange(128)
        qa[0, h] = -s * 16 * (q // 16)
        qa[1, h] = -s * (q % 16)
        qa[2, h] = 16 * s
        qa[3, h] = s
    c["swa_qaug"] = qa.astype(bf)
    ka = np.zeros((4, 128), np.float32)
    k = np.arange(128)
    ka[0] = 1
    ka[1] = 1
    ka[2] = k // 16
    ka[3] = k % 16
    c["swa_kaug"] = ka.astype(bf)
    off = np.zeros((1, 8, 128), np.float32)
    for h in range(8):
        off[0, h] = -128.0 * SLOPES[h]
    c["swa_off"] = off.astype(bf)
    da = np.zeros((8, 2, 512), np.float32)
    for h in range(8):
        s = SLOPES[h]
        q = np.arange(512)
        if h < 2:
            q = q % 128
        da[h, 0] = -s * 16 * (q // 16)
        da[h, 1] = -s * (q % 16)
    c["d_qaug"] = da.astype(bf)
    return c


def build(stage, S, dbg=False):
    NT = S // 512
    NL = NT // 4
    TOK = NL * 512
    NB = NL * 4
    nc = bass.Bass("TRN2", target_bir_lowering=False)
    R = Rec(nc)
    final = []

    def din(name, shape, dt=F32):
        return nc.dram_tensor(name, list(shape), dt, kind="ExternalInput").ap()

    def dout(name, shape, dt=F32):
        return nc.dram_tensor(name, list(shape), dt, kind="ExternalOutput").ap()

    def T(name, shape, dt=F32, psum=False):
        return Tile(nc, "t_" + name, shape, dt, psum)

    def dma(q, out_ap, in_ap, reads, writes, sem):
        R.op(q, lambda e: e.dma_start(out=out_ap, in_=in_ap), reads=reads, writes=writes, dma=sem)

    def load(q, t, src, extra_reads=()):
        dma(q, t[:], src, list(extra_reads), [t.b], t.b)

    wnames = STAGE_W[stage]
    w_in_ap = {n: din("W_" + n, [128, W_OFF[n][1]]) for n in wnames}
    wb = {n: nc.dram_tensor("WB_" + n, [128, W_OFF[n][1]], BF16).ap() for n in wnames}
    wb_bufs = {n: [Buf("wb_%s_%d" % (n, i)) for i in range((W_OFF[n][1] + CH - 1) // CH)] for n in wnames}
    NWS = 2
    wst = [T("wst%d" % i, [128, CH], F32) for i in range(NWS)]
    wbt = [T("wbt%d" % i, [128, CH], BF16) for i in range(NWS)]
    cnt = 0
    for n in wnames:
        sz = W_OFF[n][1]
        for i, o in enumerate(range(0, sz, CH)):
            c = min(CH, sz - o)
            st, bt = wst[cnt % NWS], wbt[cnt % NWS]
            dma("pool", st[:, 0:c], w_in_ap[n][:, o:o + c], [], [st.b], st.b)
            R.op("pool", lambda e, st=st, bt=bt, c=c: e.tensor_copy(out=bt[:, 0:c], in_=st[:, 0:c]), reads=[st.b], writes=[bt.b])
            dma("pool", wb[n][:, o:o + c], bt[:, 0:c], [bt.b], [wb_bufs[n][i]], bt.b)
            cnt += 1

    def wload(q, t_ap, tbuf, name, off, n):
        bufs = wb_bufs[name][off // CH:(off + n - 1) // CH + 1]
        dma(q, t_ap, wb[name][:, off:off + n], list(bufs), [tbuf], tbuf)

    def wtile(name, shape, pat=None):
        t = T("w_" + name, [128] + list(shape), BF16)
        n = int(np.prod(shape))
        flat = t[:] if len(shape) == 1 else t[:].rearrange(pat)
        wload("sp", flat, t.b, name, 0, n)
        return t

    ident = T("ident", [128, 128], BF16)
    load("sp", ident, din("c_ident", [128, 128], BF16))
    grow_d = din("grow", [GROW_TOTAL])
    grow = T("growt", [128, GROW_TOTAL], F32)
    load("sp", grow, grow_d.partition_broadcast(128))
    gcol = T("gcolt", [128, 66], F32)
    load("sp", gcol, din("gcol", [128, 66]))

    def GR(name):
        o, s = GROW_OFF[name]
        return grow[:, o:o + s]

    def GC(name):
        i = GCOL.index(name)
        return gcol[:, i * 8:(i + 1) * 8]

    epst = T("epst", [128, 1], F32)
    R.op("dve", lambda e: e.memset(epst[:], EPS), writes=[epst.b])
    ones_bf = T("ones_bf", [128, 128], BF16)
    R.op("dve", lambda e: e.memset(ones_bf[:], 1.0), writes=[ones_bf.b])

    TP = [T("TP%d" % i, [128, 1024], BF16, psum=True) for i in range(2)]
    PS = [T("PS%d" % i, [128, 512], F32, psum=True) for i in range(6)]

    junk = T("junk", [128, 1024], F32)
    xs = [T("xs%d" % i, [128, 1024], BF16) for i in range(2)]
    hT = T("hT", [128, 8, 640], BF16)
    ssb = T("ssb", [128, 8], F32)
    rsb = T("rsb", [128, 8], F32)

    def rstd_ops(out_ap, in_ap, n, obuf, ibuf, tmp):
        R.op("dve", lambda e: e.tensor_scalar(out=tmp[:, 0:in_ap.shape[1]], in0=in_ap, scalar1=1.0 / n, scalar2=None, op0=ALU.mult),
             reads=[ibuf], writes=[tmp.b])
        R.op("act", lambda e: e.activation(out=tmp[:, 0:in_ap.shape[1]], in_=tmp[:, 0:in_ap.shape[1]], func=AF.Sqrt, bias=epst[:, 0:1]),
             reads=[tmp.b, epst.b], writes=[tmp.b])
        R.op("dve", lambda e: e.reciprocal(out=out_ap, in_=tmp[:, 0:in_ap.shape[1]]), reads=[tmp.b], writes=[obuf])

    rtmp = T("rtmp", [128, 16], F32)
    tp_rr = [0]

    def rms_T(xt, blocks, gname, col0=0):
        nb_ = len(blocks)
        for i, blk in enumerate(blocks):
            R.op("act", lambda e, blk=blk, i=i: e.activation(out=junk[:], in_=xt[:, blk, :], func=AF.Square, accum_out=ssb[:, i:i + 1]),
                 reads=[xt.b], writes=[junk.b, ssb.b])
        rstd_ops(rsb[:, 0:nb_], ssb[:, 0:nb_], 1024.0, rsb.b, ssb.b, rtmp)
        gc = GC(gname)
        for i, blk in enumerate(blocks):
            x_ = xs[i % 2]
            R.op("act", lambda e, blk=blk, i=i, x_=x_: e.activation(out=x_[:], in_=xt[:, blk, :], func=AF.Copy, scale=rsb[:, i:i + 1]),
                 reads=[xt.b, rsb.b], writes=[x_.b])
            tp = TP[tp_rr[0] % 2]
            tp_rr[0] += 1
            for kc in range(8):
                R.op("pe", lambda e, kc=kc, x_=x_, tp=tp: e.transpose(out=tp[:, kc * 128:(kc + 1) * 128], in_=x_[:, kc * 128:(kc + 1) * 128], identity=ident[:]),
                     reads=[x_.b, ident.b], writes=[tp.b])
            c0 = (col0 + i) * 128
            R.op("dve", lambda e, tp=tp, c0=c0: e.tensor_tensor(out=hT[:, :, c0:c0 + 128], in0=tp[:, :].rearrange("p (k t) -> p k t", k=8),
                                                                 in1=gc.unsqueeze(2).to_broadcast([128, 8, 128]), op=ALU.mult),
                 reads=[gcol.b], writes=[tp.b, hT.b])

    def proj_tm(ps, ncols, c0h, w, wc0, nkc=8, lhs=None):
        for kc in range(nkc):
            R.op("pe", lambda e, kc=kc: e.matmul(ps[:, 0:ncols], lhsT=hT[:, kc, c0h:c0h + 128], rhs=w[:, kc, wc0:wc0 + ncols], start=(kc == 0), stop=(kc == nkc - 1)),
                 reads=[hT.b, w.b], writes=[ps.b])

    def bc(ap, axis, shape):
        return ap.unsqueeze(axis).to_broadcast(list(shape))

    PTs = [T("PT%d" % i, [128, 512], BF16) for i in range(4)]
    pt_rr = [0]

    x_in = None
    outputs = {}

    if stage == 1:
        x_d = din("x", [TOK, D])
        xh_d = din("xh", [NL * 128, D])
        pos_d = din("pos", [128, NB], I32)
        hv_d = din("hv", [128, NL])
        Oa_d = dout("Oa", [TOK, 512], BF16)
        Qm_d = dout("Qm", [NL, 8, 96, 512], BF16)
        Km_d = dout("Km", [NL * 8 * 96, 512], BF16)
        Vm_d = dout("Vm", [NL * 8 * 128, 4 * 65], BF16)
        w_in_t = wtile("w_in", [8, 1184], "p a b -> p (a b)")
        w_uq_t = wtile("w_uq", [2, 768], "p a b -> p (a b)")
        w_ukv_t = wtile("w_ukv", [1, 1024], "p a b -> p (a b)")
        mcur = T("mcur", [128, 128], BF16)
        load("sp", mcur, din("c_mcur", [128, 128], BF16))
        mprev = T("mprev", [128, 128], BF16)
        load("sp", mprev, din("c_mprev", [128, 128], BF16))
        hv = T("hv", [128, NL], F32)
        load("sp", hv, hv_d)
        onec = T("onec", [128, 1], F32)
        R.op("dve", lambda e: e.memset(onec[:], 1.0), writes=[onec.b])
        gq_s = T("gq_s", [128, 64], F32)
        R.op("dve", lambda e: e.tensor_scalar(out=gq_s[:], in0=GR("swa_qg"), scalar1=64.0 ** -0.5, scalar2=None, op0=ALU.mult), reads=[grow.b], writes=[gq_s.b])
        gmq_s = T("gmq_s", [128, 96], F32)
        R.op("dve", lambda e: e.tensor_scalar(out=gmq_s[:], in0=GR("mla_qg"), scalar1=96.0 ** -0.5, scalar2=None, op0=ALU.mult), reads=[grow.b], writes=[gmq_s.b])
        esink = T("esink", [128, 8], F32)
        R.op("act", lambda e: e.activation(out=esink[:], in_=GR("sinks"), func=AF.Exp), reads=[grow.b], writes=[esink.b])
        invn = T("invn", [128, 12], F32)
        R.op("dve", lambda e: e.memset(invn[:, 0:10], 1.0 / 64), writes=[invn.b])
        R.op("dve", lambda e: e.memset(invn[:, 10:11], 1.0 / 256), writes=[invn.b])
        R.op("dve", lambda e: e.memset(invn[:, 11:12], 1.0 / 128), writes=[invn.b])
        posi = T("posi", [128, NB], I32)
        load("sp", posi, pos_d)
        posf = T("posf", [128, NB], F32)
        R.op("dve", lambda e: e.tensor_copy(out=posf[:], in_=posi[:]), reads=[posi.b], writes=[posf.b])
        ang = T("ang", [128, NB, 16], F32)
        R.op("dve", lambda e: e.tensor_tensor(out=ang[:], in0=bc(posf[:, :], 2, [128, NB, 16]), in1=bc(GR("invf"), 1, [128, NB, 16]), op=ALU.mult),
             reads=[posf.b, grow.b], writes=[ang.b])
        SIN = T("SIN", [128, NB, 16], F32)
        COS = T("COS", [128, NB, 16], F32)
        ni = T("ni", [128, NB, 16], I32)
        nf = T("nf", [128, NB, 16], F32)
        rr_ = T("rr_", [128, NB, 16], F32)
        mm_ = T("mm_", [128, NB, 16], F32)
        TWO_PI = 2.0 * math.pi
        for (dst, shift) in ((SIN, 0.0), (COS, math.pi / 2)):
            R.op("dve", lambda e, shift=shift: e.tensor_scalar(out=nf[:], in0=ang[:], scalar1=shift, scalar2=1.0 / TWO_PI, op0=ALU.add, op1=ALU.mult), reads=[ang.b], writes=[nf.b])
            R.op("dve", lambda e: e.tensor_copy(out=ni[:], in_=nf[:]), reads=[nf.b], writes=[ni.b])
            R.op("dve", lambda e: e.tensor_copy(out=nf[:], in_=ni[:]), reads=[ni.b], writes=[nf.b])
            R.op("dve", lambda e, shift=shift: e.tensor_scalar(out=rr_[:], in0=ang[:], scalar1=shift, scalar2=None, op0=ALU.add), reads=[ang.b], writes=[rr_.b])
            R.op("dve", lambda e: e.scalar_tensor_tensor(out=rr_[:], in0=nf[:], scalar=-TWO_PI, in1=rr_[:], op0=ALU.mult, op1=ALU.add), reads=[nf.b, rr_.b], writes=[rr_.b])
            R.op("dve", lambda e: e.tensor_scalar(out=mm_[:], in0=rr_[:], scalar1=math.pi, scalar2=None, op0=ALU.is_gt), reads=[rr_.b], writes=[mm_.b])
            R.op("dve", lambda e: e.scalar_tensor_tensor(out=rr_[:], in0=mm_[:], scalar=-TWO_PI, in1=rr_[:], op0=ALU.mult, op1=ALU.add), reads=[mm_.b, rr_.b], writes=[rr_.b])
            R.op("dve", lambda e: e.tensor_scalar(out=mm_[:], in0=rr_[:], scalar1=-math.pi, scalar2=None, op0=ALU.is_lt), reads=[rr_.b], writes=[mm_.b])
            R.op("dve", lambda e: e.scalar_tensor_tensor(out=rr_[:], in0=mm_[:], scalar=TWO_PI, in1=rr_[:], op0=ALU.mult, op1=ALU.add), reads=[mm_.b, rr_.b], writes=[rr_.b])
            R.op("dve", lambda e: e.tensor_scalar(out=rr_[:], in0=rr_[:], scalar1=3.1415925, scalar2=-3.1415925, op0=ALU.min, op1=ALU.max), reads=[rr_.b], writes=[rr_.b])
            R.op("act", lambda e, dst=dst: e.activation(out=dst[:], in_=rr_[:], func=AF.Sin), reads=[rr_.b], writes=[dst.b])

        QaT = T("QaT", [68, 8, 128], BF16)
        KaT = T("KaT", [68, 2, 5, 128], BF16)
        Va = T("Va", [128, 5, 2, 65], BF16)
        offrow = T("offrow", [1, 8, 128], BF16)
        load("sp", offrow, din("c_swa_off", [1, 8, 128], BF16))
        dma("sp", QaT[64:68, :, :], din("c_swa_qaug", [4, 8, 128], BF16), [], [QaT.b], QaT.b)
        kaug_d = din("c_swa_kaug", [4, 128], BF16)
        for kvh in range(2):
            for blk in range(5):
                dma("sp", KaT[64:68, kvh, blk, :], kaug_d, [], [KaT.b], KaT.b)
        R.op("dve", lambda e: e.memset(Va[:], 1.0), writes=[Va.b])
        QmT = T("QmT", [96, 8, 512], BF16)
        KmT = T("KmT", [96, 8, 512], BF16)
        Vm = T("Vm_t", [128, 8, 4, 65], BF16)
        R.op("dve", lambda e: e.memset(Vm[:], 1.0), writes=[Vm.b])
        Oa_t = T("Oa_t", [128, 4, 512], BF16)
        xts = [T("xt%d" % i, [128, 5, 1024], F32) for i in range(2)]
        sq = T("sq", [128, 1024], F32)
        ssg = T("ssg", [128, 12], F32)
        rsg = T("rsg", [128, 12], F32)
        ssg2 = T("ssg2", [128, 16], F32)
        rs2 = T("rs2", [128, 16], F32)
        s_a = T("s_a", [128, 8], F32)
        s_b = T("s_b", [128, 8], F32)
        s_c = T("s_c", [128, 1], F32)
        tmpA = T("tmpA", [128, 512], F32)
        tmpB = T("tmpB", [128, 256], F32)
        qa_n = T("qa_n", [128, 512], BF16)
        ka_n = T("ka_n", [128, 128], BF16)
        cn = T("cn", [128, 384], BF16)
        cT = T("cT", [128, 3, 128], BF16)
        Qtm = T("Qtm", [128, 8, 96], BF16)
        Ktm = T("Ktm", [128, 8, 96], BF16)
        krg = T("krg", [128, 32], F32)
        krr = T("krr", [128, 32], F32)
        r_a = T("r_a", [128, 8, 16], F32)
        r_b = T("r_b", [128, 8, 16], F32)
        den = T("den", [128, 8], F32)
        PA, PB, PC, PD, PE_, PF = PS

        def rope(dst1, dst2, x1, x2, blk, nh, rbufs, wbuf):
            cs = bc(COS[:, blk, :], 1, [128, nh, 16]) if nh > 1 else COS[:, blk, :]
            sn = bc(SIN[:, blk, :], 1, [128, nh, 16]) if nh > 1 else SIN[:, blk, :]
            ra = r_a[:, 0:nh, :] if nh > 1 else r_a[:, 0, :]
            rb = r_b[:, 0:nh, :] if nh > 1 else r_b[:, 0, :]
            R.op("dve", lambda e: e.tensor_tensor(out=ra, in0=x1, in1=cs, op=ALU.mult), reads=rbufs + [COS.b], writes=[r_a.b])
            R.op("pool", lambda e: e.tensor_tensor(out=rb, in0=x2, in1=sn, op=ALU.mult), reads=rbufs + [SIN.b], writes=[r_b.b])
            R.op("dve", lambda e: e.tensor_tensor(out=dst1, in0=ra, in1=rb, op=ALU.subtract), reads=[r_a.b, r_b.b], writes=[wbuf])
            R.op("dve", lambda e: e.tensor_tensor(out=ra, in0=x2, in1=cs, op=ALU.mult), reads=rbufs + [COS.b], writes=[r_a.b])
            R.op("pool", lambda e: e.tensor_tensor(out=rb, in0=x1, in1=sn, op=ALU.mult), reads=rbufs + [SIN.b], writes=[r_b.b])
            R.op("dve", lambda e: e.tensor_tensor(out=dst2, in0=ra, in1=rb, op=ALU.add), reads=[r_a.b, r_b.b], writes=[wbuf])

        for lt in range(NL):
            xt = xts[lt % 2]
            dma("sp", xt[:, 0:4, :], x_d[lt * 512:(lt + 1) * 512, :].rearrange("(t p) c -> p t c", p=128), [], [xt.b], xt.b)
            dma("sp", xt[:, 4, :], xh_d[lt * 128:(lt + 1) * 128, :], [], [xt.b], xt.b)
            order = [4, 0, 1, 2, 3]
            rms_T(xt, order, "mix0")
            for i, blk in enumerate(order):
                halo = blk == 4
                c0h = i * 128
                kslot = 0 if halo else blk + 1
                gblk = lt * 4 + (0 if halo else blk)
                if not halo:
                    proj_tm(PA, 512, c0h, w_in_t, 0)
                    proj_tm(PB, 512, c0h, w_in_t, 512)
                    proj_tm(PC, 160, c0h, w_in_t, 1024)
                else:
                    proj_tm(PB, 256, c0h, w_in_t, 512)
                if not halo:
                    R.op("act", lambda e: e.activation(out=sq[:, 0:512], in_=PA[:, :], func=AF.Square), writes=[PA.b, sq.b])
                    R.op("dve", lambda e: e.tensor_reduce(out=ssg[:, 0:8], in_=sq[:, 0:512].rearrange("p (h d) -> p h d", h=8), axis=AX.X, op=ALU.add), reads=[sq.b], writes=[ssg.b])
                R.op("act", lambda e: e.activation(out=sq[:, 512:640], in_=PB[:, 0:128], func=AF.Square), writes=[PB.b, sq.b])
                R.op("dve", lambda e: e.tensor_reduce(out=ssg[:, 8:10], in_=sq[:, 512:640].rearrange("p (h d) -> p h d", h=2), axis=AX.X, op=ALU.add), reads=[sq.b], writes=[ssg.b])
                if not halo:
                    R.op("act", lambda e: e.activation(out=junk[:, 0:256], in_=PB[:, 256:512], func=AF.Square, accum_out=ssg[:, 10:11]), writes=[PB.b, junk.b, ssg.b])
                    R.op("act", lambda e: e.activation(out=junk[:, 0:128], in_=PC[:, 0:128], func=AF.Square, accum_out=ssg[:, 11:12]), writes=[PC.b, junk.b, ssg.b])
                lo, hi = (8, 10) if halo else (0, 12)
                R.op("dve", lambda e, lo=lo, hi=hi: e.tensor_tensor(out=rtmp[:, lo:hi], in0=ssg[:, lo:hi], in1=invn[:, lo:hi], op=ALU.mult), reads=[ssg.b, invn.b], writes=[rtmp.b])
                R.op("act", lambda e, lo=lo, hi=hi: e.activation(out=rtmp[:, lo:hi], in_=rtmp[:, lo:hi], func=AF.Sqrt, bias=epst[:, 0:1]), reads=[rtmp.b, epst.b], writes=[rtmp.b])
                R.op("dve", lambda e, lo=lo, hi=hi: e.reciprocal(out=rsg[:, lo:hi], in_=rtmp[:, lo:hi]), reads=[rtmp.b], writes=[rsg.b])
                R.op("dve", lambda e: e.tensor_tensor(out=tmpB[:, 0:128].rearrange("p (h d) -> p h d", h=2), in0=PB[:, 0:128].rearrange("p (h d) -> p h d", h=2),
                                                       in1=bc(rsg[:, 8:10], 2, [128, 2, 64]), op=ALU.mult), reads=[rsg.b], writes=[PB.b, tmpB.b])
                R.op("pool", lambda e: e.tensor_tensor(out=ka_n[:, :].rearrange("p (h d) -> p h d", h=2), in0=tmpB[:, 0:128].rearrange("p (h d) -> p h d", h=2),
                                                        in1=bc(GR("swa_kg"), 1, [128, 2, 64]), op=ALU.mult), reads=[tmpB.b, grow.b], writes=[ka_n.b])
                R.op("act", lambda e, kslot=kslot: e.copy(out=Va[:, kslot, :, 0:64], in_=PB[:, 128:256].rearrange("p (h d) -> p h d", h=2)), writes=[PB.b, Va.b])
                tpb = TP[1]
                for kvh in range(2):
                    R.op("pe", lambda e, kvh=kvh: e.transpose(out=tpb[0:64, kvh * 128:(kvh + 1) * 128], in_=ka_n[:, kvh * 64:(kvh + 1) * 64], identity=ident[:]),
                         reads=[ka_n.b, ident.b], writes=[tpb.b])
                if halo:
                    R.op("dve", lambda e, kslot=kslot: e.tensor_copy(out=KaT[0:64, :, kslot, :], in_=tpb[0:64, 0:256].rearrange("p (h t) -> p h t", h=2)), writes=[tpb.b, KaT.b])
                    continue
                R.op("dve", lambda e: e.tensor_tensor(out=tmpA[:, :].rearrange("p (h d) -> p h d", h=8), in0=PA[:, :].rearrange("p (h d) -> p h d", h=8),
                                                       in1=bc(rsg[:, 0:8], 2, [128, 8, 64]), op=ALU.mult), reads=[rsg.b], writes=[PA.b, tmpA.b])
                R.op("pool", lambda e: e.tensor_tensor(out=qa_n[:, :].rearrange("p (h d) -> p h d", h=8), in0=tmpA[:, :].rearrange("p (h d) -> p h d", h=8),
                                                        in1=bc(gq_s[:, :], 1, [128, 8, 64]), op=ALU.mult), reads=[tmpA.b, gq_s.b], writes=[qa_n.b])
                R.op("dve", lambda e: e.scalar_tensor_tensor(out=cn[:, 0:256], in0=PB[:, 256:512], scalar=rsg[:, 10:11], in1=GR("qlat"), op0=ALU.mult, op1=ALU.mult),
                     reads=[rsg.b, grow.b], writes=[PB.b, cn.b])
                R.op("dve", lambda e: e.scalar_tensor_tensor(out=cn[:, 256:384], in0=PC[:, 0:128], scalar=rsg[:, 11:12], in1=GR("kvlat"), op0=ALU.mult, op1=ALU.mult),
                     reads=[rsg.b, grow.b], writes=[PC.b, cn.b])
                R.op("dve", lambda e: e.tensor_tensor(out=krg[:], in0=PC[:, 128:160], in1=GR("mla_kg")[:, 64:96], op=ALU.mult), reads=[grow.b], writes=[PC.b, krg.b])
                R.op("act", lambda e: e.activation(out=junk[:, 0:32], in_=PC[:, 128:160], func=AF.Square, accum_out=s_c[:, 0:1]), writes=[PC.b, junk.b, s_c.b])
                tpa = TP[0]
                for h in range(8):
                    R.op("pe", lambda e, h=h: e.transpose(out=tpa[0:64, h * 128:(h + 1) * 128], in_=qa_n[:, h * 64:(h + 1) * 64], identity=ident[:]),
                         reads=[qa_n.b, ident.b], writes=[tpa.b])
                for j in range(3):
                    R.op("pe", lambda e, j=j: e.transpose(out=tpb[:, 256 + j * 128:256 + (j + 1) * 128], in_=cn[:, j * 128:(j + 1) * 128], identity=ident[:]),
                         reads=[cn.b, ident.b], writes=[tpb.b])
                R.op("act", lambda e: e.copy(out=QaT[0:64, :, :], in_=tpa[0:64, :].rearrange("p (h t) -> p h t", h=8)), writes=[tpa.b, QaT.b])
                R.op("dve", lambda e, kslot=kslot: e.tensor_copy(out=KaT[0:64, :, kslot, :], in_=tpb[0:64, 0:256].rearrange("p (h t) -> p h t", h=2)), writes=[tpb.b, KaT.b])
                R.op("dve", lambda e: e.tensor_copy(out=cT[:, :, :], in_=tpb[:, 256:640].rearrange("p (j t) -> p j t", j=3)), writes=[tpb.b, cT.b])
                for kc in range(2):
                    R.op("pe", lambda e, kc=kc: e.matmul(PD[:, 0:512], lhsT=cT[:, kc, :], rhs=w_uq_t[:, kc, 0:512], start=(kc == 0), stop=(kc == 1)), reads=[cT.b, w_uq_t.b], writes=[PD.b])
                for kc in range(2):
                    R.op("pe", lambda e, kc=kc: e.matmul(PE_[:, 0:256], lhsT=cT[:, kc, :], rhs=w_uq_t[:, kc, 512:768], start=(kc == 0), stop=(kc == 1)), reads=[cT.b, w_uq_t.b], writes=[PE_.b])
                R.op("pe", lambda e: e.matmul(PA[:, 0:512], lhsT=cT[:, 2, :], rhs=w_ukv_t[:, 0, 0:512], start=True, stop=True), reads=[cT.b, w_ukv_t.b], writes=[PA.b])
                R.op("pe", lambda e: e.matmul(PB[:, 0:512], lhsT=cT[:, 2, :], rhs=w_ukv_t[:, 0, 512:1024], start=True, stop=True), reads=[cT.b, w_ukv_t.b], writes=[PB.b])
                R.op("act", lambda e: e.activation(out=sq[:, 0:512], in_=PD[:, :], func=AF.Square), writes=[PD.b, sq.b])
                R.op("dve", lambda e: e.tensor_reduce(out=s_a[:], in_=sq[:, 0:512].rearrange("p (h d) -> p h d", h=8), axis=AX.X, op=ALU.add), reads=[sq.b], writes=[s_a.b])
                R.op("act", lambda e: e.activation(out=sq[:, 512:768], in_=PE_[:, 0:256], func=AF.Square), writes=[PE_.b, sq.b])
                R.op("dve", lambda e: e.tensor_reduce(out=s_b[:], in_=sq[:, 512:768].rearrange("p (h d) -> p h d", h=8), axis=AX.X, op=ALU.add), reads=[sq.b], writes=[s_b.b])
                R.op("dve", lambda e: e.tensor_tensor(out=ssg2[:, 0:8], in0=s_a[:], in1=s_b[:], op=ALU.add), reads=[s_a.b, s_b.b], writes=[ssg2.b])
                R.op("act", lambda e: e.activation(out=sq[:, 0:512], in_=PA[:, :], func=AF.Square), writes=[PA.b, sq.b])
                R.op("dve", lambda e: e.tensor_reduce(out=s_a[:], in_=sq[:, 0:512].rearrange("p (h d) -> p h d", h=8), axis=AX.X, op=ALU.add), reads=[sq.b], writes=[s_a.b])
                R.op("dve", lambda e: e.tensor_scalar(out=ssg2[:, 8:16], in0=s_a[:], scalar1=s_c[:, 0:1], scalar2=None, op0=ALU.add), reads=[s_a.b, s_c.b], writes=[ssg2.b])
                rstd_ops(rs2[:, 0:16], ssg2[:, 0:16], 96.0, rs2.b, ssg2.b, rtmp)
                R.op("dve", lambda e: e.tensor_tensor(out=tmpA[:, :].rearrange("p (h d) -> p h d", h=8), in0=PD[:, :].rearrange("p (h d) -> p h d", h=8),
                                                       in1=bc(rs2[:, 0:8], 2, [128, 8, 64]), op=ALU.mult), reads=[rs2.b], writes=[PD.b, tmpA.b])
                R.op("pool", lambda e: e.tensor_tensor(out=Qtm[:, :, 0:64], in0=tmpA[:, :].rearrange("p (h d) -> p h d", h=8),
                                                        in1=bc(gmq_s[:, 0:64], 1, [128, 8, 64]), op=ALU.mult), reads=[tmpA.b, gmq_s.b], writes=[Qtm.b])
                R.op("dve", lambda e: e.tensor_tensor(out=tmpB[:, :].rearrange("p (h d) -> p h d", h=8), in0=PE_[:, 0:256].rearrange("p (h d) -> p h d", h=8),
                                                       in1=bc(rs2[:, 0:8], 2, [128, 8, 32]), op=ALU.mult), reads=[rs2.b], writes=[PE_.b, tmpB.b])
                R.op("pool", lambda e: e.tensor_tensor(out=tmpB[:, :].rearrange("p (h d) -> p h d", h=8), in0=tmpB[:, :].rearrange("p (h d) -> p h d", h=8),
                                                        in1=bc(gmq_s[:, 64:96], 1, [128, 8, 32]), op=ALU.mult), reads=[tmpB.b, gmq_s.b], writes=[tmpB.b])
                tB3 = tmpB[:, :].rearrange("p (h d) -> p h d", h=8)
                rope(Qtm[:, :, 64:80], Qtm[:, :, 80:96], tB3[:, :, 0:16], tB3[:, :, 16:32], gblk, 8, [tmpB.b], Qtm.b)
                R.op("dve", lambda e: e.tensor_tensor(out=tmpA[:, :].rearrange("p (h d) -> p h d", h=8), in0=PA[:, :].rearrange("p (h d) -> p h d", h=8),
                                                       in1=bc(rs2[:, 8:16], 2, [128, 8, 64]), op=ALU.mult), reads=[rs2.b], writes=[PA.b, tmpA.b])
                R.op("pool", lambda e: e.tensor_tensor(out=Ktm[:, :, 0:64], in0=tmpA[:, :].rearrange("p (h d) -> p h d", h=8),
                                                        in1=bc(GR("mla_kg")[:, 0:64], 1, [128, 8, 64]), op=ALU.mult), reads=[tmpA.b, grow.b], writes=[Ktm.b])
                rope(krr[:, 0:16], krr[:, 16:32], krg[:, 0:16], krg[:, 16:32], gblk, 1, [krg.b], krr.b)
                R.op("dve", lambda e: e.tensor_tensor(out=Ktm[:, :, 64:96], in0=bc(krr[:, :], 1, [128, 8, 32]), in1=bc(rs2[:, 8:16], 2, [128, 8, 32]), op=ALU.mult),
                     reads=[krr.b, rs2.b], writes=[Ktm.b])
                R.op("act", lambda e, blk=blk: e.copy(out=Vm[:, :, blk, 0:64], in_=PB[:, :].rearrange("p (h d) -> p h d", h=8)), writes=[PB.b, Vm.b])
                for h in range(8):
                    R.op("pe", lambda e, h=h: e.transpose(out=tpa[0:96, h * 128:(h + 1) * 128], in_=Qtm[:, h, :], identity=ident[:]), reads=[Qtm.b, ident.b], writes=[tpa.b])
                for h in range(8):
                    R.op("pe", lambda e, h=h: e.transpose(out=tpb[0:96, h * 128:(h + 1) * 128], in_=Ktm[:, h, :], identity=ident[:]), reads=[Ktm.b, ident.b], writes=[tpb.b])
                R.op("act", lambda e, blk=blk: e.copy(out=QmT[:, :, blk * 128:(blk + 1) * 128], in_=tpa[0:96, :].rearrange("p (h t) -> p h t", h=8)), writes=[tpa.b, QmT.b])
                R.op("dve", lambda e, blk=blk: e.tensor_copy(out=KmT[:, :, blk * 128:(blk + 1) * 128], in_=tpb[0:96, :].rearrange("p (h t) -> p h t", h=8)), writes=[tpb.b, KmT.b])
                for kvh in range(2):
                    ACC = PD if kvh == 0 else PE_
                    first = True
                    for role in (0, 1):
                        ks = kslot - 1 if role == 0 else kslot
                        R.op("pe", lambda e, kvh=kvh, ks=ks, role=role: e.matmul(PF[:, :], lhsT=KaT[0:68, kvh, ks, :], rhs=QaT[0:68, kvh * 4:(kvh + 1) * 4, :],
                                                                                  start=True, stop=(role == 1)), reads=[KaT.b, QaT.b], writes=[PF.b])
                        if role == 0:
                            R.op("pe", lambda e, kvh=kvh: e.matmul(PF[:, :], lhsT=ones_bf[0:1, 0:128], rhs=offrow[0:1, kvh * 4:(kvh + 1) * 4, :], start=False, stop=True),
                                 reads=[ones_bf.b, offrow.b], writes=[PF.b])
                        pt = PTs[pt_rr[0] % 4]
                        pt_rr[0] += 1
                        if dbg and lt == 0 and blk == 1:
                            dS = T("dS%d%d" % (kvh, role), [128, 512], F32)
                            R.op("dve", lambda e, dS=dS: e.tensor_copy(out=dS[:], in_=PF[:, :]), writes=[PF.b, dS.b])
                            dma("pool", dout("dbgS%d%d" % (kvh, role), [128, 512]), dS[:], [dS.b], [Buf("x")], dS.b)
                            final.append(dS.b)
                        R.op("act", lambda e, pt=pt: e.activation(out=pt[:], in_=PF[:, :], func=AF.Exp), writes=[PF.b, pt.b])
                        p3 = pt[:, :].rearrange("p (h t) -> p h t", h=4)
                        if role == 0:
                            sc = hv[:, lt:lt + 1] if blk == 0 else onec[:, 0:1]
                            R.op("dve", lambda e, p3=p3, sc=sc: e.scalar_tensor_tensor(out=p3, in0=p3, scalar=sc, in1=bc(mprev[:, :], 1, [128, 4, 128]), op0=ALU.mult, op1=ALU.mult),
                                 reads=[pt.b, mprev.b, hv.b, onec.b], writes=[pt.b])
                        else:
                            R.op("dve", lambda e, p3=p3: e.tensor_tensor(out=p3, in0=p3, in1=bc(mcur[:, :], 1, [128, 4, 128]), op=ALU.mult), reads=[pt.b, mcur.b], writes=[pt.b])
                        for hh in range(4):
                            R.op("pe", lambda e, hh=hh, pt=pt, kvh=kvh, ks=ks, first=first, role=role: e.matmul(
                                ACC[:, hh * 65:(hh + 1) * 65], lhsT=pt[:, hh * 128:(hh + 1) * 128], rhs=Va[:, ks, kvh, :], start=first, stop=(role == 1), skip_group_check=True),
                                reads=[pt.b, Va.b], writes=[ACC.b])
                            first = False
                    a3 = ACC[:, 0:260].rearrange("p (h c) -> p h c", h=4)
                    if dbg and lt == 0 and blk == 1:
                        dA = T("dA%d" % kvh, [128, 260], F32)
                        R.op("dve", lambda e, dA=dA, ACC=ACC: e.tensor_copy(out=dA[:], in_=ACC[:, 0:260]), writes=[ACC.b, dA.b])
                        dma("pool", dout("dbgA%d" % kvh, [128, 260]), dA[:], [dA.b], [Buf("x")], dA.b)
                        final.append(dA.b)
                    R.op("dve", lambda e, a3=a3, kvh=kvh: e.tensor_tensor(out=den[:, kvh * 4:(kvh + 1) * 4], in0=a3[:, :, 64], in1=esink[:, kvh * 4:(kvh + 1) * 4], op=ALU.add),
                         reads=[esink.b], writes=[ACC.b, den.b])
                    R.op("dve", lambda e, kvh=kvh: e.reciprocal(out=den[:, kvh * 4:(kvh + 1) * 4], in_=den[:, kvh * 4:(kvh + 1) * 4]), reads=[den.b], writes=[den.b])
                    R.op("dve", lambda e, a3=a3, kvh=kvh, blk=blk: e.tensor_tensor(out=Oa_t[:, blk, kvh * 256:(kvh + 1) * 256].rearrange("p (h d) -> p h d", h=4), in0=a3[:, :, 0:64],
                                                                                    in1=bc(den[:, kvh * 4:(kvh + 1) * 4], 2, [128, 4, 64]), op=ALU.mult),
                         reads=[den.b], writes=[ACC.b, Oa_t.b])
            ob = Buf("o")
            dma("pool", Oa_d[lt * 512:(lt + 1) * 512, :].rearrange("(t p) c -> p t c", p=128), Oa_t[:], [Oa_t.b], [ob], Oa_t.b)
            dma("pool", Qm_d[lt].rearrange("h d t -> d h t"), QmT[:], [QmT.b], [ob], QmT.b)
            dma("pool", Km_d[lt * 768:(lt + 1) * 768, :].rearrange("(h d) t -> d h t", h=8), KmT[:], [KmT.b], [ob], KmT.b)
            dma("pool", Vm_d[lt * 1024:(lt + 1) * 1024, :].rearrange("(h p) c -> p h c", h=8), Vm[:].rearrange("p h k c -> p h (k c)"), [Vm.b], [ob], Vm.b)
            final.extend([Oa_t.b, QmT.b, KmT.b, Vm.b])

    if stage in (2, 3):
        L = stage - 2
        x_d = din("x", [TOK, D])
        mem_d = din("mem", [MEMT, D])
        qidx_d = din("qidx", [TOK])
        kidx_d = din("kidx", [128, NT * 4])
        q0_d = din("q0", [128, NL])
        xo_d = dout("xo", [TOK, D])
        kidx = T("kidx", [128, NT * 4], F32)
        load("sp", kidx, kidx_d)
        q0t = T("q0t", [128, NL], F32)
        load("sp", q0t, q0_d)
        if stage == 2:
            Oa_d = din("Oa", [TOK, 512], BF16)
            Q_d = din("Qm", [NL, 8, 96, 512], BF16)
            K_d = din("Kall", [4 * NL * 8 * 96, 512], BF16)
            V_d = din("Vall", [4 * NL * 8 * 128, 4 * 65], BF16)
            Qd_o = dout("Qd", [NL, 8, 2, 64, 512], BF16)
            Kd_o = dout("Kd", [NL * 8 * 2 * 64, 512], BF16)
            Vd_o = dout("Vd", [NL * 8 * 128, 4 * 129], BF16)
            WO = "ev_wo"
            DK, DV, NCOMP = 96, 64, 1
        else:
            Q_d = din("Qd", [NL, 8, 2, 64, 512], BF16)
            K_d = din("Kall", [4 * NL * 8 * 2 * 64, 512], BF16)
            V_d = din("Vall", [4 * NL * 8 * 128, 4 * 129], BF16)
            qaug_d = din("c_d_qaug", [8, 2, 512], BF16)
            WO = "od_wo"
            DK, DV, NCOMP = 66, 128, 2
        sfx = "%d" % L
        S0, S1, A0, A1, A2, A3 = PS
        ACCS = [A0, A1, A2, A3]
        wchs = [T("wch%d" % i, [128, 8, 512], BF16) for i in range(3)]
        wch_rr = [0]

        def wstream(name, kdim, ntot, c0):
            t = wchs[wch_rr[0] % 3]
            wch_rr[0] += 1
            src = wb[name].rearrange("p (k n) -> p k n", k=kdim)[:, :, c0:c0 + 512]
            lo = c0
            hi = (kdim - 1) * ntot + c0 + 512
            bufs = wb_bufs[name][lo // CH:(hi - 1) // CH + 1]
            dma("sp", t[:, 0:kdim, :], src, list(bufs), [t.b], t.b)
            return t

        xt = T("xt", [128, 4, 1024], F32)
        dma("sp", xt[:, 0:2, :], mem_d.rearrange("(t p) c -> p t c", p=128), [], [xt.b], xt.b)
        rms_T(xt, [0, 1], "mkvn" + sfx)
        KmemT = T("KmemT", [128, 4, 256], BF16)
        Vmem = T("Vmem", [128, 2, 4, 129], BF16)
        R.op("dve", lambda e: e.memset(Vmem[:], 1.0), writes=[Vmem.b])
        sqm = T("sqm", [128, 512], F32)
        ssm = T("ssm", [128, 4], F32)
        rsm = T("rsm", [128, 4], F32)
        tmpM = T("tmpM", [128, 512], F32)
        kmn = T("kmn", [128, 512], BF16)
        wk_ = wstream("mkv" + sfx, 8, 1024, 0)
        wv_ = wstream("mkv" + sfx, 8, 1024, 512)
        for mb in range(2):
            proj_tm(S0, 512, mb * 128, wk_, 0)
            proj_tm(S1, 512, mb * 128, wv_, 0)
            R.op("act", lambda e: e.activation(out=sqm[:], in_=S0[:, :], func=AF.Square), writes=[S0.b, sqm.b])
            R.op("dve", lambda e: e.tensor_reduce(out=ssm[:], in_=sqm[:, :].rearrange("p (h d) -> p h d", h=4), axis=AX.X, op=ALU.add), reads=[sqm.b], writes=[ssm.b])
            rstd_ops(rsm[:, 0:4], ssm[:, 0:4], 128.0, rsm.b, ssm.b, rtmp)
            R.op("dve", lambda e: e.tensor_tensor(out=tmpM[:, :].rearrange("p (h d) -> p h d", h=4), in0=S0[:, :].rearrange("p (h d) -> p h d", h=4),
                                                   in1=bc(rsm[:, 0:4], 2, [128, 4, 128]), op=ALU.mult), reads=[rsm.b], writes=[S0.b, tmpM.b])
            R.op("pool", lambda e: e.tensor_tensor(out=kmn[:, :].rearrange("p (h d) -> p h d", h=4), in0=tmpM[:, :].rearrange("p (h d) -> p h d", h=4),
                                                    in1=bc(GR("memk" + sfx), 1, [128, 4, 128]), op=ALU.mult), reads=[tmpM.b, grow.b], writes=[kmn.b])
            R.op("act", lambda e, mb=mb: e.copy(out=Vmem[:, mb, :, 0:128], in_=S1[:, :].rearrange("p (h d) -> p h d", h=4)), writes=[S1.b, Vmem.b])
            tp = TP[0]
            for h in range(4):
                R.op("pe", lambda e, h=h: e.transpose(out=tp[:, h * 128:(h + 1) * 128], in_=kmn[:, h * 128:(h + 1) * 128], identity=ident[:]), reads=[kmn.b, ident.b], writes=[tp.b])
            R.op("dve", lambda e, mb=mb: e.tensor_copy(out=KmemT[:, :, mb * 128:(mb + 1) * 128], in_=tp[:, 0:512].rearrange("p (h t) -> p h t", h=4)), writes=[tp.b, KmemT.b])
        gmq_col = T("gmq_col", [128, 1], F32)
        R.op("dve", lambda e: e.tensor_scalar(out=gmq_col[:], in0=gcol[:, 64 + L:65 + L], scalar1=128.0 ** -0.5, scalar2=None, op0=ALU.mult), reads=[gcol.b], writes=[gmq_col.b])

        Ot = T("Ot", [128, 4, 1024], BF16)
        OT = T("OT", [128, 8, 512], BF16)
        AT = T("AT", [128, NFC, 512], BF16)
        qidxb = T("qidxb", [128, 512], F32)
        QTs = [T("QT%d" % i, [DK if stage == 2 else 66, NCOMP, 512], BF16) for i in range(2)]
        NKV = 6
        KTs = [T("KT%d" % i, [DK if stage == 2 else 66, NCOMP, 512], BF16) for i in range(NKV)]
        VTs = [T("VT%d" % i, [128, 4, DV + 1], BF16) for i in range(NKV)]
        if stage == 3:
            for kt in KTs:
                R.op("dve", lambda e, kt=kt: e.memset(kt[64:66, :, :], 1.0), writes=[kt.b])
        rec4 = T("rec4", [128, 8], F32)
        ball = T("ball", [128, 4, NT * 4], F32)
        NGU = 3 if stage == 3 else 2
        gus = [T("gu%d" % i, [128, 2, 8, 128], BF16) for i in range(NGU)]
        wds = [T("wd%d" % i, [128, 2, 512], BF16) for i in range(3)]
        sgs = [T("sg%d" % i, [128, 512], F32) for i in range(2)]
        qTm = T("qTm", [128, 4, 512], BF16)
        sqb = T("sqb", [128, 512], BF16)
        rstf = T("rstf", [128, 512], F32)
        Om = T("Om", [128, 4, 512], BF16)
        OmT = T("OmT", [128, 4, 512], BF16)
        kv_rr = [0]
        q_rr = [0]
        acc_rr = [0]
        if stage == 3:
            lamt = T("lamt", [128, 4], F32)
            lt1 = T("lt1", [128, 128], F32)
            lam_ap = GR("lam")
            R.op("dve", lambda e: e.tensor_tensor(out=lt1[:, 0:64], in0=lam_ap[:, 0:64], in1=lam_ap[:, 64:128], op=ALU.mult), reads=[grow.b], writes=[lt1.b])
            R.op("dve", lambda e: e.tensor_tensor(out=lt1[:, 64:128], in0=lam_ap[:, 128:192], in1=lam_ap[:, 192:256], op=ALU.mult), reads=[grow.b], writes=[lt1.b])
            R.op("dve", lambda e: e.tensor_reduce(out=lamt[:, 0:2], in_=lt1[:, :].rearrange("p (a d) -> p a d", a=2), axis=AX.X, op=ALU.add), reads=[lt1.b], writes=[lamt.b])
            R.op("act", lambda e: e.activation(out=lamt[:, 0:2], in_=lamt[:, 0:2], func=AF.Exp), reads=[lamt.b], writes=[lamt.b])
            R.op("dve", lambda e: e.tensor_tensor(out=lamt[:, 2:3], in0=lamt[:, 1:2], in1=lamt[:, 0:1], op=ALU.subtract), reads=[lamt.b], writes=[lamt.b])
            R.op("dve", lambda e: e.tensor_scalar(out=lamt[:, 3:4], in0=lamt[:, 2:3], scalar1=-LAMBDA_INIT, scalar2=None, op0=ALU.add), reads=[lamt.b], writes=[lamt.b])
            gsub = T("gsub", [128, 128], F32)
            R.op("dve", lambda e: e.tensor_scalar(out=gsub[:], in0=GR("subln"), scalar1=(1.0 - LAMBDA_INIT), scalar2=None, op0=ALU.mult), reads=[grow.b], writes=[gsub.b])
            dd = T("dd", [128, 4, 128], F32)
            t1 = T("t1", [128, 128], F32)
            rr2 = T("rr2", [128, 2], F32)
            ssd = T("ssd", [128, 4], F32)
            rsd = T("rsd", [128, 4], F32)
        else:
            gq_s = T("gq_s", [128, 64], F32)
            R.op("dve", lambda e: e.tensor_scalar(out=gq_s[:], in0=GR("od_qg"), scalar1=64.0 ** -0.5, scalar2=None, op0=ALU.mult), reads=[grow.b], writes=[gq_s.b])
            QKst = [T("QKst%d" % i, [64, 8, 512], BF16) for i in range(2)]
            Vst = [T("Vst%d" % i, [128, 4, 4, 129], BF16) for i in range(2)]
            for v_ in Vst:
                R.op("dve", lambda e, v_=v_: e.memset(v_[:], 1.0), writes=[v_.b])
            sqq = T("sqq", [128, 512], F32)
            ssq = T("ssq", [128, 8], F32)
            rsq = T("rsq", [128, 8], F32)
            tmpQ = T("tmpQ", [128, 512], F32)
            Qtm2 = T("Qtm2", [128, 512], BF16)

        Kall_b = Buf("Kall")
        Vall_b = Buf("Vall")

        def attn_steps(QT, steps, nq, fin_cb):
            pend = None
            nsteps = len(steps)
            loaded = [0]

            def ensure(upto):
                while loaded[0] < min(upto, nsteps):
                    lf = steps[loaded[0]].get("load")
                    if lf is not None:
                        lf()
                    loaded[0] += 1
            for si, st in enumerate(steps + [None]):
                cur = None
                if st is not None:
                    ensure(si + 17)
                    pts = []
                    for c in range(NCOMP if st["kind"] == "dense" else 1):
                        Sb = (S0, S1)[(st["sidx"] + c) % 2]
                        R.op("pe", lambda e, st=st, c=c, Sb=Sb: e.matmul(Sb[:, 0:nq], lhsT=st["kt"][c], rhs=st["qt"][c], start=True, stop=True), reads=st["rd_qk"], writes=[Sb.b])
                        pt = PTs[pt_rr[0] % 4]
                        pt_rr[0] += 1
                        if st.get("steep"):
                            for qb in range(4):
                                R.op("act", lambda e, pt=pt, Sb=Sb, qb=qb, st=st: e.activation(out=pt[:, qb * 128:(qb + 1) * 128], in_=Sb[:, qb * 128:(qb + 1) * 128], func=AF.Exp,
                                                                                                bias=st["bias"][qb]), reads=st["rd_b"], writes=[Sb.b, pt.b])
                        elif st.get("bias") is not None:
                            R.op("act", lambda e, pt=pt, Sb=Sb, st=st: e.activation(out=pt[:, 0:nq], in_=Sb[:, 0:nq], func=AF.Exp, bias=st["bias"]), reads=st["rd_b"], writes=[Sb.b, pt.b])
                        else:
                            R.op("act", lambda e, pt=pt, Sb=Sb: e.activation(out=pt[:, 0:nq], in_=Sb[:, 0:nq], func=AF.Exp), writes=[Sb.b, pt.b])
                        if st.get("mask") is not None:
                            R.op("dve", lambda e, pt=pt, st=st: e.scalar_tensor_tensor(out=pt[:, 0:nq], in0=qidxb[:, 0:nq], scalar=st["mask"], in1=pt[:, 0:nq], op0=ALU.is_ge, op1=ALU.mult),
                                 reads=[qidxb.b, kidx.b, pt.b], writes=[pt.b])
                        pts.append(pt)
                    cur = (st, pts)
                if pend is not None:
                    pst, ppts = pend
                    for c, pt in enumerate(ppts):
                        for qb in range(nq // 128):
                            acc_ap, accb, startf = pst["acc"](c, qb)
                            R.op("pe", lambda e, pt=pt, qb=qb, acc_ap=acc_ap, startf=startf, pst=pst: e.matmul(acc_ap, lhsT=pt[:, qb * 128:(qb + 1) * 128], rhs=pst["v"], start=startf,
                                                                                                       stop=pst["last"], skip_group_check=True), reads=[pt.b] + pst["rd_v"], writes=[accb])
                    if pst["last"]:
                        fin_cb(pst)
                pend = cur

        for lt in range(NL):
            p = lt // 2
            gmin = 8 * p if lt % 2 == 0 else 8 * p + 4
            gmax = gmin + 3
            dma("sp", xt[:], x_d[lt * 512:(lt + 1) * 512, :].rearrange("(t p) c -> p t c", p=128), [], [xt.b], xt.b)
            dma("sp", qidxb[:], qidx_d[lt * 512:(lt + 1) * 512].partition_broadcast(128), [], [qidxb.b], qidxb.b)
            if stage == 2:
                dma("sp", Ot[:, :, 0:512], Oa_d[lt * 512:(lt + 1) * 512, :].rearrange("(t p) c -> p t c", p=128), [], [Ot.b], Ot.b)
            for h in range(8):
                QT = QTs[q_rr[0] % 2]
                q_rr[0] += 1
                if stage == 2:
                    dma("sp", QT[:, 0, :], Q_d[lt, h], [], [QT.b], QT.b)
                else:
                    dma("sp", QT[0:64, :, :], Q_d[lt, h].rearrange("c d t -> d c t"), [], [QT.b], QT.b)
                    for c in range(2):
                        dma("sp", QT[64:66, c, :], qaug_d[h], [], [QT.b], QT.b)
                steep = (stage == 3 and h < 2)
                if stage == 3:
                    s = SLOPES[h]
                    nqb = 4 if steep else 1
                    for qb in range(nqb):
                        R.op("dve", lambda e, qb=qb, s=s: e.tensor_scalar(out=ball[:, qb, :], in0=kidx[:, :], scalar1=q0t[:, lt:lt + 1], scalar2=s, op0=ALU.subtract, op1=ALU.mult),
                             reads=[kidx.b, q0t.b], writes=[ball.b])
                        cap = s * (127.0 if steep else 511.0)
                        R.op("dve", lambda e, qb=qb, s=s, cap=cap: e.tensor_scalar(out=ball[:, qb, :], in0=ball[:, qb, :], scalar1=-128.0 * qb * s, scalar2=cap, op0=ALU.add, op1=ALU.min),
                             reads=[ball.b], writes=[ball.b])
                if stage == 2:
                    ACC = ACCS[acc_rr[0] % 4]
                    acc_rr[0] += 1
                steps = []
                glist = []
                for g in range(0, gmax + 1):
                    if stage == 3 and g < gmin - 1:
                        dmin = (gmin - g - 1) * 512 + 1
                        if SLOPES[h] * dmin >= 110.0:
                            continue
                    glist.append(g)
                started = {}
                for gi, g in enumerate(glist):
                    jj, ltk = owner(g)
                    slot = jj * NL + ltk
                    KT = KTs[kv_rr[0] % NKV]
                    VT = VTs[kv_rr[0] % NKV]
                    kv_rr[0] += 1
                    if stage == 2:
                        def loadf(KT=KT, VT=VT, slot=slot, h=h):
                            dma("sp", KT[:, 0, :], K_d[(slot * 8 + h) * 96:(slot * 8 + h + 1) * 96, :], [Kall_b], [KT.b], KT.b)
                            dma("sp", VT[:].rearrange("p k c -> p (k c)"), V_d[(slot * 8 + h) * 128:(slot * 8 + h + 1) * 128, :], [Vall_b], [VT.b], VT.b)
                    else:
                        def loadf(KT=KT, VT=VT, slot=slot, h=h):
                            dma("sp", KT[0:64, :, :], K_d[(slot * 8 + h) * 128:(slot * 8 + h + 1) * 128, :].rearrange("(c d) t -> d c t", c=2), [Kall_b], [KT.b], KT.b)
                            dma("sp", VT[:].rearrange("p k c -> p (k c)"), V_d[(slot * 8 + h) * 128:(slot * 8 + h + 1) * 128, :], [Vall_b], [VT.b], VT.b)
                    dep = g >= gmin
                    for kb in range(4):
                        col = slot * 4 + kb
                        st = {"kind": "dense", "sidx": (gi * 4 + kb) * NCOMP}
                        st["load"] = loadf if kb == 0 else None
                        st["kt"] = [KT[:, c, kb * 128:(kb + 1) * 128] for c in range(NCOMP)]
                        st["qt"] = [QT[:, c, :] for c in range(NCOMP)]
                        st["rd_qk"] = [KT.b, QT.b]
                        st["v"] = VT[:, kb, :]
                        st["rd_v"] = [VT.b]
                        st["mask"] = kidx[:, col:col + 1] if dep else None
                        if stage == 3:
                            st["steep"] = steep
                            st["rd_b"] = [ball.b]
                            st["bias"] = [ball[:, qb, col:col + 1] for qb in range(4)] if steep else ball[:, 0, col:col + 1]
                        else:
                            st["bias"] = None
                        st["last"] = (gi == len(glist) - 1 and kb == 3)
                        st["h"] = h
                        if stage == 2:
                            def accf(c, qb, ACC=ACC, started=started):
                                f = not started.get("a", False)
                                started["a"] = True
                                return ACC[:, qb * 65:(qb + 1) * 65], ACC.b, f
                        else:
                            def accf(c, qb, started=started):
                                A = ACCS[qb]
                                f = not started.get(qb, False)
                                started[qb] = True
                                return A[:, c * 129:(c + 1) * 129], A.b, f
                        st["acc"] = accf
                        if stage == 2:
                            st["ACC"] = ACC
                        steps.append(st)

                def fin_cb(pst):
                    hh = pst["h"]
                    if stage == 2:
                        A = pst["ACC"]
                        a3 = A[:, 0:260].rearrange("p (q c) -> p q c", q=4)
                        R.op("dve", lambda e, a3=a3: e.reciprocal(out=rec4[:, 0:4], in_=a3[:, :, 64]), writes=[A.b, rec4.b])
                        R.op("dve", lambda e, a3=a3, hh=hh: e.tensor_tensor(out=Ot[:, :, 512 + hh * 64:512 + (hh + 1) * 64], in0=a3[:, :, 0:64], in1=bc(rec4[:, 0:4], 2, [128, 4, 64]), op=ALU.mult),
                             reads=[rec4.b], writes=[A.b, Ot.b])
                    else:
                        for qb in range(4):
                            A = ACCS[qb]
                            a3 = A[:, 0:258].rearrange("p (c d) -> p c d", c=2)
                            R.op("dve", lambda e, a3=a3: e.reciprocal(out=rr2[:, 0:2], in_=a3[:, :, 128]), writes=[A.b, rr2.b])
                            R.op("dve", lambda e: e.tensor_tensor(out=rr2[:, 1:2], in0=rr2[:, 1:2], in1=lamt[:, 3:4], op=ALU.mult), reads=[rr2.b, lamt.b], writes=[rr2.b])
                            R.op("act", lambda e, a3=a3: e.activation(out=t1[:], in_=a3[:, 0, 0:128], func=AF.Copy, scale=rr2[:, 0:1]), reads=[rr2.b], writes=[A.b, t1.b])
                            R.op("dve", lambda e, a3=a3, qb=qb: e.scalar_tensor_tensor(out=dd[:, qb, :], in0=a3[:, 1, 0:128], scalar=rr2[:, 1:2], in1=t1[:], op0=ALU.mult, op1=ALU.add),
                                 reads=[rr2.b, t1.b], writes=[A.b, dd.b])
                            R.op("act", lambda e, qb=qb: e.activation(out=junk[:, 0:128], in_=dd[:, qb, :], func=AF.Square, accum_out=ssd[:, qb:qb + 1]), reads=[dd.b], writes=[junk.b, ssd.b])
                        rstd_ops(rsd[:, 0:4], ssd[:, 0:4], 128.0, rsd.b, ssd.b, rtmp)
                        for qb in range(4):
                            R.op("dve", lambda e, qb=qb, hh=hh: e.scalar_tensor_tensor(out=Ot[:, qb, hh * 128:(hh + 1) * 128], in0=dd[:, qb, :], scalar=rsd[:, qb:qb + 1], in1=gsub[:],
                                                                                       op0=ALU.mult, op1=ALU.mult), reads=[dd.b, rsd.b, gsub.b], writes=[Ot.b])
                attn_steps(QT, steps, 512, fin_cb)

            for tb in range(4):
                tp = TP[tp_rr[0] % 2]
                tp_rr[0] += 1
                for kc in range(8):
                    R.op("pe", lambda e, kc=kc, tb=tb, tp=tp: e.transpose(out=tp[:, kc * 128:(kc + 1) * 128], in_=Ot[:, tb, kc * 128:(kc + 1) * 128], identity=ident[:]),
                         reads=[Ot.b, ident.b], writes=[tp.b])
                R.op("act", lambda e, tb=tb, tp=tp: e.copy(out=OT[:, :, tb * 128:(tb + 1) * 128], in_=tp[:, :].rearrange("p (k t) -> p k t", k=8)), writes=[tp.b, OT.b])
            k_ = 0
            for ng in range(2):
                wo_c = wstream(WO, 8, 1024, ng * 512)
                for tb in range(4):
                    ps = PS[k_ % 6]
                    k_ += 1
                    for kc in range(8):
                        R.op("pe", lambda e, kc=kc, tb=tb, ps=ps, wo_c=wo_c: e.matmul(ps[:, :], lhsT=OT[:, kc, tb * 128:(tb + 1) * 128], rhs=wo_c[:, kc, :],
                                                                                      start=(kc == 0), stop=(kc == 7)), reads=[OT.b, wo_c.b], writes=[ps.b])
                    R.op("dve", lambda e, tb=tb, ng=ng, ps=ps: e.tensor_tensor(out=xt[:, tb, ng * 512:(ng + 1) * 512], in0=xt[:, tb, ng * 512:(ng + 1) * 512], in1=ps[:, :], op=ALU.add),
                         reads=[xt.b], writes=[ps.b, xt.b])
            if dbg:
                if lt == 0:
                    dx1 = dout("dbg_x1", [TOK, D])
                    dx2 = dout("dbg_x2", [TOK, D])
                    dOt = dout("dbg_Ot", [TOK, D], BF16)
                dma("pool", dx1[lt * 512:(lt + 1) * 512, :].rearrange("(t p) c -> p t c", p=128), xt[:], [xt.b], [Buf("x")], xt.b)
                dma("pool", dOt[lt * 512:(lt + 1) * 512, :].rearrange("(t p) c -> p t c", p=128), Ot[:], [Ot.b], [Buf("x")], Ot.b)
                final.append(Ot.b)
            rms_T(xt, [0, 1, 2, 3], "mqn" + sfx)
            mq_t = wstream("mq" + sfx, 8, 512, 0)
            for h in range(4):
                ps = PS[2 + h % 2]
                for kc in range(8):
                    R.op("pe", lambda e, kc=kc, h=h, ps=ps, mq_t=mq_t: e.matmul(ps[:, :], lhsT=mq_t[:, kc, h * 128:(h + 1) * 128], rhs=hT[:, kc, 0:512], start=(kc == 0), stop=(kc == 7)),
                         reads=[mq_t.b, hT.b], writes=[ps.b])
                R.op("act", lambda e, ps=ps: e.activation(out=sqb[:], in_=ps[:, :], func=AF.Square), writes=[ps.b, sqb.b])
                ps2 = PS[4 + h % 2]
                R.op("pe", lambda e, ps2=ps2: e.matmul(ps2[:, :], lhsT=ones_bf[:, :], rhs=sqb[:, :], start=True, stop=True), reads=[ones_bf.b, sqb.b], writes=[ps2.b])
                R.op("dve", lambda e, ps2=ps2: e.tensor_scalar(out=rstf[:], in0=ps2[:, :], scalar1=1.0 / 128, scalar2=None, op0=ALU.mult), writes=[ps2.b, rstf.b])
                R.op("act", lambda e: e.activation(out=rstf[:], in_=rstf[:], func=AF.Sqrt, bias=epst[:, 0:1]), reads=[rstf.b, epst.b], writes=[rstf.b])
                R.op("dve", lambda e: e.reciprocal(out=rstf[:], in_=rstf[:]), reads=[rstf.b], writes=[rstf.b])
                R.op("dve", lambda e, h=h, ps=ps: e.scalar_tensor_tensor(out=qTm[:, h, :], in0=ps[:, :], scalar=gmq_col[:, 0:1], in1=rstf[:], op0=ALU.mult, op1=ALU.mult),
                     reads=[gmq_col.b, rstf.b], writes=[ps.b, qTm.b])
            for h in range(4):
                started = {}
                steps = []
                for mb in range(2):
                    st = {"kind": "mem", "sidx": mb, "kt": [KmemT[:, h, mb * 128:(mb + 1) * 128]], "qt": [qTm[:, h, :]], "rd_qk": [KmemT.b, qTm.b],
                          "v": Vmem[:, mb, h, :], "rd_v": [Vmem.b], "mask": None, "bias": None, "last": mb == 1, "h": h}

                    def accf(c, qb, started=started):
                        A = ACCS[qb]
                        f = not started.get(qb, False)
                        started[qb] = True
                        return A[:, 0:129], A.b, f
                    st["acc"] = accf
                    steps.append(st)

                def fin_mem(pst):
                    hh = pst["h"]
                    for qb in range(4):
                        A = ACCS[qb]
                        R.op("dve", lambda e, A=A, qb=qb: e.reciprocal(out=rec4[:, qb:qb + 1], in_=A[:, 128:129]), writes=[A.b, rec4.b])
                        R.op("act", lambda e, A=A, qb=qb, hh=hh: e.activation(out=Om[:, qb, hh * 128:(hh + 1) * 128], in_=A[:, 0:128], func=AF.Copy, scale=rec4[:, qb:qb + 1]),
                             reads=[rec4.b], writes=[A.b, Om.b])
                attn_steps(None, steps, 512, fin_mem)
            for tb in range(4):
                tp = TP[tp_rr[0] % 2]
                tp_rr[0] += 1
                for hh in range(4):
                    R.op("pe", lambda e, hh=hh, tb=tb, tp=tp: e.transpose(out=tp[:, hh * 128:(hh + 1) * 128], in_=Om[:, tb, hh * 128:(hh + 1) * 128], identity=ident[:]),
                         reads=[Om.b, ident.b], writes=[tp.b])
                R.op("act", lambda e, tb=tb, tp=tp: e.copy(out=OmT[:, :, tb * 128:(tb + 1) * 128], in_=tp[:, 0:512].rearrange("p (k t) -> p k t", k=4)), writes=[tp.b, OmT.b])
            k_ = 0
            for ng in range(2):
                mo_c = wstream("mo" + sfx, 4, 1024, ng * 512)
                for tb in range(4):
                    ps = PS[k_ % 6]
                    k_ += 1
                    for hh in range(4):
                        R.op("pe", lambda e, hh=hh, tb=tb, ps=ps, mo_c=mo_c: e.matmul(ps[:, :], lhsT=OmT[:, hh, tb * 128:(tb + 1) * 128], rhs=mo_c[:, hh, :],
                                                                                      start=(hh == 0), stop=(hh == 3)), reads=[OmT.b, mo_c.b], writes=[ps.b])
                    R.op("dve", lambda e, tb=tb, ng=ng, ps=ps: e.tensor_tensor(out=xt[:, tb, ng * 512:(ng + 1) * 512], in0=xt[:, tb, ng * 512:(ng + 1) * 512], in1=ps[:, :], op=ALU.add),
                         reads=[xt.b], writes=[ps.b, xt.b])
            if dbg:
                dma("pool", dx2[lt * 512:(lt + 1) * 512, :].rearrange("(t p) c -> p t c", p=128), xt[:], [xt.b], [Buf("x")], xt.b)
            rms_T(xt, [0, 1, 2, 3], "ffn" + sfx)
            gun = "gu" + sfx
            wdn = "wd" + sfx
            for fc in range(NFC):
                gu = gus[fc % NGU]
                wload("sp", gu[:].rearrange("p a b c -> p (a b c)"), gu.b, gun, fc * 2048, 2048)
                G = PS[(2 * fc) % 6]
                U = PS[(2 * fc + 1) % 6]
                for kc in range(8):
                    R.op("pe", lambda e, kc=kc, gu=gu, G=G: e.matmul(G[:, :], lhsT=gu[:, 0, kc, :], rhs=hT[:, kc, 0:512], start=(kc == 0), stop=(kc == 7)), reads=[gu.b, hT.b], writes=[G.b])
                for kc in range(8):
                    R.op("pe", lambda e, kc=kc, gu=gu, U=U: e.matmul(U[:, :], lhsT=gu[:, 1, kc, :], rhs=hT[:, kc, 0:512], start=(kc == 0), stop=(kc == 7)), reads=[gu.b, hT.b], writes=[U.b])
                sg = sgs[fc % 2]
                R.op("act", lambda e, sg=sg, G=G: e.activation(out=sg[:], in_=G[:, :], func=AF.Silu), writes=[G.b, sg.b])
                R.op("dve", lambda e, sg=sg, U=U, fc=fc: e.tensor_tensor(out=AT[:, fc, :], in0=sg[:], in1=U[:, :], op=ALU.mult), reads=[sg.b], writes=[U.b, AT.b])
            ci = 0
            for ng in range(2):
                for fcp in range(11):
                    wd = wds[ci % 3]
                    ci += 1
                    wload("sp", wd[:].rearrange("p a b -> p (a b)"), wd.b, wdn, (ng * 11 + fcp) * 1024, 1024)
                    for fci in range(2):
                        fc = 2 * fcp + fci
                        for tb in range(4):
                            A = ACCS[tb]
                            R.op("pe", lambda e, fc=fc, fci=fci, tb=tb, A=A, wd=wd: e.matmul(A[:, :], lhsT=AT[:, fc, tb * 128:(tb + 1) * 128], rhs=wd[:, fci, :], start=(fc == 0), stop=(fc == NFC - 1)),
                                 reads=[AT.b, wd.b], writes=[A.b])
                for tb in range(4):
                    A = ACCS[tb]
                    R.op("dve", lambda e, tb=tb, ng=ng, A=A: e.tensor_tensor(out=xt[:, tb, ng * 512:(ng + 1) * 512], in0=xt[:, tb, ng * 512:(ng + 1) * 512], in1=A[:, :], op=ALU.add),
                         reads=[xt.b], writes=[A.b, xt.b])
            ob = Buf("xo")
            dma("pool", xo_d[lt * 512:(lt + 1) * 512, :].rearrange("(t p) c -> p t c", p=128), xt[:], [xt.b], [ob], xt.b)
            final.append(xt.b)
            if stage == 2:
                rms_T(xt, [0, 1, 2, 3], "mix1")
                for grp in range(6):
                    wq_c = wstream("qkv", 8, 3072, grp * 512)
                    if grp < 4:
                        stg = QKst[grp % 2]
                    else:
                        stg = Vst[grp % 2]
                    for tb in range(4):
                        ps = PS[(grp * 4 + tb) % 6]
                        proj_tm(ps, 512, tb * 128, wq_c, 0)
                        if grp < 4:
                            gain = gq_s[:, :] if grp < 2 else GR("od_kg")
                            gbuf = gq_s.b if grp < 2 else grow.b
                            R.op("act", lambda e, ps=ps: e.activation(out=sqq[:], in_=ps[:, :], func=AF.Square), writes=[ps.b, sqq.b])
                            R.op("dve", lambda e: e.tensor_reduce(out=ssq[:], in_=sqq[:, :].rearrange("p (h d) -> p h d", h=8), axis=AX.X, op=ALU.add), reads=[sqq.b], writes=[ssq.b])
                            rstd_ops(rsq[:, 0:8], ssq[:, 0:8], 64.0, rsq.b, ssq.b, rtmp)
                            R.op("dve", lambda e, ps=ps: e.tensor_tensor(out=tmpQ[:, :].rearrange("p (h d) -> p h d", h=8), in0=ps[:, :].rearrange("p (h d) -> p h d", h=8),
                                                                         in1=bc(rsq[:, 0:8], 2, [128, 8, 64]), op=ALU.mult), reads=[rsq.b], writes=[ps.b, tmpQ.b])
                            R.op("pool", lambda e, gain=gain: e.tensor_tensor(out=Qtm2[:, :].rearrange("p (h d) -> p h d", h=8), in0=tmpQ[:, :].rearrange("p (h d) -> p h d", h=8),
                                                                              in1=bc(gain, 1, [128, 8, 64]), op=ALU.mult), reads=[tmpQ.b, gbuf], writes=[Qtm2.b])
                            tp = TP[tp_rr[0] % 2]
                            tp_rr[0] += 1
                            for i8 in range(8):
                                R.op("pe", lambda e, i8=i8, tp=tp: e.transpose(out=tp[0:64, i8 * 128:(i8 + 1) * 128], in_=Qtm2[:, i8 * 64:(i8 + 1) * 64], identity=ident[:]),
                                     reads=[Qtm2.b, ident.b], writes=[tp.b])
                            R.op("act", lambda e, tp=tp, stg=stg, tb=tb: e.copy(out=stg[:, :, tb * 128:(tb + 1) * 128], in_=tp[0:64, :].rearrange("p (h t) -> p h t", h=8)),
                                 writes=[tp.b, stg.b])
                        else:
                            R.op("act", lambda e, ps=ps, stg=stg, tb=tb: e.copy(out=stg[:, :, tb, 0:128], in_=ps[:, :].rearrange("p (h d) -> p h d", h=4)), writes=[ps.b, stg.b])
                    ob2 = Buf("o2")
                    if grp < 2:
                        dma("pool", Qd_o[lt, grp * 4:(grp + 1) * 4].rearrange("h c d t -> d (h c) t"), stg[:], [stg.b], [ob2], stg.b)
                    elif grp < 4:
                        g2 = grp - 2
                        dma("pool", Kd_o[lt * 1024 + g2 * 512:lt * 1024 + (g2 + 1) * 512, :].rearrange("(hc d) t -> d hc t", d=64), stg[:], [stg.b], [ob2], stg.b)
                    else:
                        g2 = grp - 4
                        dma("pool", Vd_o[lt * 1024 + g2 * 512:lt * 1024 + (g2 + 1) * 512, :].rearrange("(h p) c -> p h c", h=4), stg[:].rearrange("p h k c -> p h (k c)"), [stg.b], [ob2], stg.b)
                    final.append(stg.b)

    R.emit(final_bufs=final)
    return nc


_DBG = None
_STOP_AFTER = 99


def _run(nc, in_maps):
    res = run_bass_kernel_spmd(nc, in_maps, core_ids=list(range(8)))
    return res.results


def kernel(**inputs):
    x = np.asarray(inputs["x"], np.float32)
    mem = np.asarray(inputs["mem"], np.float32)
    pos = np.asarray(inputs["positions"])
    B, S, _ = x.shape
    NT = S // 512
    NL = NT // 4
    TOK = NL * 512
    NB = NL * 4
    Wb, grow, gcol = host_prep(inputs)
    C = const_tables()
    core_tiles = [[gtile(c % 4, lt) for lt in range(NL)] for c in range(8)]

    def shard_rows(arr_b, tiles):
        return np.ascontiguousarray(np.concatenate([arr_b[g * 512:(g + 1) * 512] for g in tiles], axis=0))

    xs, xhs, poss, hvs, qidxs, q0s = [], [], [], [], [], []
    for c in range(8):
        b = c // 4
        tiles = core_tiles[c]
        xs.append(shard_rows(x[b], tiles))
        halo = []
        hvv = np.ones((128, NL), np.float32)
        for i, g in enumerate(tiles):
            if g == 0:
                halo.append(np.zeros((128, D), np.float32))
                hvv[:, i] = 0.0
            else:
                halo.append(x[b, g * 512 - 128:g * 512])
        xhs.append(np.ascontiguousarray(np.concatenate(halo, axis=0)))
        hvs.append(hvv)
        pl = shard_rows(pos[b], tiles).astype(np.int32)
        poss.append(np.ascontiguousarray(pl.reshape(NB, 128).T))
        qi = np.concatenate([np.arange(g * 512, (g + 1) * 512) for g in tiles]).astype(np.float32)
        qidxs.append(qi)
        q0s.append(np.ascontiguousarray(np.broadcast_to(np.asarray([g * 512 for g in tiles], np.float32)[None, :], (128, NL))))
    kidx = np.zeros((128, NT * 4), np.float32)
    for jj in range(4):
        for ltk in range(NL):
            g = gtile(jj, ltk)
            for kb in range(4):
                kidx[:, (jj * NL + ltk) * 4 + kb] = g * 512 + kb * 128 + np.arange(128)

    def common(c):
        return {"c_ident": C["ident"], "grow": grow, "gcol": gcol}

    nc1 = build(1, S, dbg=(_DBG is not None))
    in1 = []
    for c in range(8):
        m = common(c)
        m.update({"x": xs[c], "xh": xhs[c], "pos": poss[c], "hv": hvs[c], "c_mcur": C["mcur"], "c_mprev": C["mprev"],
                  "c_swa_off": C["swa_off"], "c_swa_qaug": C["swa_qaug"], "c_swa_kaug": C["swa_kaug"]})
        for n in STAGE_W[1]:
            m["W_" + n] = Wb[n]
        in1.append(m)
    r1 = _run(nc1, in1)
    if _DBG is not None:
        _DBG["r1"] = r1
        _DBG["core_tiles"] = core_tiles
    if _STOP_AFTER <= 1:
        return None
    nc2 = build(2, S, dbg=(_DBG is not None))
    in2 = []
    for c in range(8):
        b = c // 4
        grp = range(4 * b, 4 * b + 4)
        m = common(c)
        m.update({"x": xs[c], "mem": mem[b], "qidx": qidxs[c], "kidx": kidx, "q0": q0s[c],
                  "Oa": r1[c]["Oa"], "Qm": r1[c]["Qm"],
                  "Kall": np.concatenate([r1[k]["Km"] for k in grp], axis=0),
                  "Vall": np.concatenate([r1[k]["Vm"] for k in grp], axis=0)})
        for n in STAGE_W[2]:
            m["W_" + n] = Wb[n]
        in2.append(m)
    r2 = _run(nc2, in2)
    if _DBG is not None:
        _DBG["r2"] = r2
    if _STOP_AFTER <= 2:
        return None
    nc3 = build(3, S)
    in3 = []
    for c in range(8):
        b = c // 4
        grp = range(4 * b, 4 * b + 4)
        m = common(c)
        m.update({"x": r2[c]["xo"], "mem": mem[b], "qidx": qidxs[c], "kidx": kidx, "q0": q0s[c],
                  "Qd": r2[c]["Qd"], "c_d_qaug": C["d_qaug"],
                  "Kall": np.concatenate([r2[k]["Kd"] for k in grp], axis=0),
                  "Vall": np.concatenate([r2[k]["Vd"] for k in grp], axis=0)})
        for n in STAGE_W[3]:
            m["W_" + n] = Wb[n]
        in3.append(m)
    r3 = _run(nc3, in3)
    out = np.zeros((B, S, D), np.float32)
    for c in range(8):
        b = c // 4
        for i, g in enumerate(core_tiles[c]):
            out[b, g * 512:(g + 1) * 512] = r3[c]["xo"][i * 512:(i + 1) * 512]
    return out
```

```python
import math
import types
import numpy as np
import ml_dtypes
import concourse.bass as bass
import concourse.mybir as mybir
from concourse.bass_utils import run_bass_kernel_spmd

F32 = mybir.dt.float32
BF16 = mybir.dt.bfloat16
I32 = mybir.dt.int32
AF = mybir.ActivationFunctionType
ALU = mybir.AluOpType
AX = mybir.AxisListType
bf = ml_dtypes.bfloat16

SAME_ENGINE_SYNC = True
EPS = 1e-6
D = 1024
DFF = 2816
NFC = 22
MEMT = 256


class Buf:
    __slots__ = ("name", "lw", "rd", "sem", "cnt")

    def __init__(self, name):
        self.name = name
        self.lw = None
        self.rd = []
        self.sem = None
        self.cnt = 0


class Op:
    __slots__ = ("eng", "fn", "deps", "dmabuf", "dmaval", "need_sig", "sigval", "dmawaits", "idx")


class Rec:
    ENGS = ("pe", "act", "dve", "pool", "sp")

    def __init__(self, nc):
        self.nc = nc
        self.ops = {e: [] for e in self.ENGS}
        self.n = 0

    def op(self, eng, fn, reads=(), writes=(), dma=None):
        o = Op()
        o.eng = eng
        if fn is not None and fn.__closure__:
            cells = []
            for c in fn.__closure__:
                try:
                    cells.append(types.CellType(c.cell_contents))
                except ValueError:
                    cells.append(c)
            f2 = types.FunctionType(fn.__code__, fn.__globals__, fn.__name__, fn.__defaults__, tuple(cells))
            f2.__kwdefaults__ = fn.__kwdefaults__
            fn = f2
        o.fn = fn
        o.dmabuf = dma
        o.need_sig = False
        o.sigval = None
        o.idx = self.n
        self.n += 1
        deps = {}
        for b in reads:
            if b.lw is not None:
                deps[id(b.lw)] = b.lw
        for b in writes:
            if b.lw is not None:
                deps[id(b.lw)] = b.lw
            for r in b.rd:
                deps[id(r)] = r
        o.deps = []
        o.dmawaits = []
        for d in deps.values():
            if d.dmabuf is not None:
                o.dmawaits.append((d.dmabuf, d.dmabuf.cnt * 16))
            else:
                if d.eng == eng and (eng == "pe" or not SAME_ENGINE_SYNC):
                    continue
                d.need_sig = True
                o.deps.append(d)
        if dma is not None:
            if dma.sem is None:
                dma.sem = self.nc.alloc_semaphore("ds_%d" % self.n)
            dma.cnt += 1
            o.dmaval = dma.cnt * 16
            assert o.dmaval < 65000, ("dma sem overflow", dma.name)
        for b in reads:
            b.rd.append(o)
        for b in writes:
            b.lw = o
            b.rd = []
        self.ops[eng].append(o)
        return o

    def emit(self, final_bufs=()):
        nc = self.nc
        esem = {e: nc.alloc_semaphore("es_" + e) for e in ("pe", "act", "dve", "pool")}
        for e in ("pe", "act", "dve", "pool"):
            c = 0
            for o in self.ops[e]:
                if o.need_sig and o.dmabuf is None:
                    c += 1
                    o.sigval = c
            assert c < 65000, ("engine sem overflow", e, c)
        fin = Op()
        fin.eng = "sp"
        fin.fn = None
        fin.deps = []
        fin.dmabuf = None
        fin.need_sig = False
        seen_b = {}
        for b in final_bufs:
            if b.sem is not None:
                seen_b[id(b)] = (b, b.cnt * 16)
        fin.dmawaits = list(seen_b.values())
        self.ops["sp"].append(fin)

        def run(ename, eng):
            seen = {}
            for o in self.ops[ename]:
                for d in o.deps:
                    key = ("e", d.eng)
                    if seen.get(key, 0) < d.sigval:
                        eng.wait_ge(esem[d.eng], d.sigval)
                        seen[key] = d.sigval
                for (b, v) in o.dmawaits:
                    key = ("d", id(b))
                    if seen.get(key, 0) < v:
                        eng.wait_ge(b.sem, v)
                        seen[key] = v
                if o.fn is None:
                    continue
                ins = o.fn(eng)
                if o.dmabuf is not None:
                    ins.then_inc(o.dmabuf.sem, 16)
                elif o.need_sig:
                    ins.then_inc(esem[ename], 1)

        with nc.Block() as block:
            @block.tensor
            def _(e):
                run("pe", e)

            @block.scalar
            def _(e):
                run("act", e)

            @block.vector
            def _(e):
                run("dve", e)

            @block.gpsimd
            def _(e):
                run("pool", e)

            @block.sync
            def _(e):
                run("sp", e)


class Tile:
    def __init__(self, nc, name, shape, dtype, psum=False):
        if psum:
            self.t = nc.alloc_psum_tensor(name, list(shape), dtype)
        else:
            self.t = nc.alloc_sbuf_tensor(name, list(shape), dtype)
        self.b = Buf(name)
        self.shape = tuple(shape)

    def __getitem__(self, k):
        return self.t[k]


def gtile(j, lt):
    p = lt // 2
    return 8 * p + (j if lt % 2 == 0 else 7 - j)


def owner(g):
    p, o = g // 8, g % 8
    return (o, 2 * p) if o < 4 else (7 - o, 2 * p + 1)


SLOPES = [2.0 ** (-(i + 1)) for i in range(8)]
LAMBDA_INIT = 0.8 - 0.6 * math.exp(-0.3 * 1)

W_SPECS = [
    ("w_in", 8 * 1184), ("w_uq", 2 * 768), ("w_ukv", 1024),
    ("ev_wo", 8 * 1024), ("mq0", 8 * 512), ("mkv0", 8 * 1024), ("mo0", 4 * 1024),
    ("gu0", NFC * 2 * 8 * 128), ("wd0", NFC * 1024),
    ("qkv", 8 * 3072),
    ("od_wo", 8 * 1024), ("mq1", 8 * 512), ("mkv1", 8 * 1024), ("mo1", 4 * 1024),
    ("gu1", NFC * 2 * 8 * 128), ("wd1", NFC * 1024),
]
W_OFF = {}
_o = 0
for _n, _s in W_SPECS:
    W_OFF[_n] = (_o, _s)
    _o += _s
W_TOTAL = _o
STAGE_W = {1: ["w_in", "w_uq", "w_ukv"],
           2: ["ev_wo", "mq0", "mkv0", "mo0", "gu0", "wd0", "qkv"],
           3: ["od_wo", "mq1", "mkv1", "mo1", "gu1", "wd1"]}
CH = 1024

GROW = [("swa_qg", 64), ("swa_kg", 64), ("qlat", 256), ("kvlat", 128), ("mla_qg", 96), ("mla_kg", 96),
        ("od_qg", 64), ("od_kg", 64), ("subln", 128), ("memk0", 128), ("memk1", 128),
        ("sinks", 8), ("lam", 256), ("invf", 16)]
GROW_OFF = {}
_o = 0
for _n, _s in GROW:
    GROW_OFF[_n] = (_o, _s)
    _o += _s
GROW_TOTAL = _o
GCOL = ["mix0", "mix1", "mqn0", "mqn1", "mkvn0", "mkvn1", "ffn0", "ffn1"]


def wrearr(w):
    K, N = w.shape
    return np.ascontiguousarray(w.reshape(K // 128, 128, N).transpose(1, 0, 2).reshape(128, -1))


def host_prep(inp):
    f = lambda a: np.asarray(a, dtype=np.float32)
    Wb = {}
    Wb["w_in"] = wrearr(f(inp["ev_w_in"])[0])
    wuq = f(inp["ev_w_uq"])[0].reshape(256, 8, 96)
    Wb["w_uq"] = wrearr(np.concatenate([wuq[:, :, :64].reshape(256, 512), wuq[:, :, 64:].reshape(256, 256)], axis=1))
    wukv = f(inp["ev_w_ukv"])[0].reshape(128, 8, 128)
    Wb["w_ukv"] = wrearr(np.concatenate([wukv[:, :, :64].reshape(128, 512), wukv[:, :, 64:].reshape(128, 512)], axis=1))
    Wb["ev_wo"] = wrearr(f(inp["ev_w_out"])[0])
    Wb["qkv"] = wrearr(f(inp["od_w_qkv"])[0])
    Wb["od_wo"] = wrearr(f(inp["od_w_out"])[0])
    for l in range(2):
        Wb["mq%d" % l] = wrearr(f(inp["mem_w_q"])[l])
        Wb["mkv%d" % l] = wrearr(f(inp["mem_w_kv"])[l])
        Wb["mo%d" % l] = wrearr(f(inp["mem_w_out"])[l])
        g = f(inp["ffn_w_gate"])[l].reshape(8, 128, NFC, 128)
        u = f(inp["ffn_w_up"])[l].reshape(8, 128, NFC, 128)
        gu = np.stack([g, u], axis=0)
        Wb["gu%d" % l] = np.ascontiguousarray(gu.transpose(2, 3, 0, 1, 4).reshape(128, -1))
        wd = f(inp["ffn_w_down"])[l].reshape(11, 2, 128, 2, 512)
        Wb["wd%d" % l] = np.ascontiguousarray(wd.transpose(2, 3, 0, 1, 4).reshape(128, -1))
    for n, s in W_SPECS:
        assert Wb[n].shape == (128, s), (n, Wb[n].shape, s)
    rows = {"swa_qg": f(inp["ev_swa_q_gain"])[0], "swa_kg": f(inp["ev_swa_k_gain"])[0],
            "qlat": f(inp["ev_q_latent_norm"])[0], "kvlat": f(inp["ev_kv_latent_norm"])[0],
            "mla_qg": f(inp["ev_mla_q_gain"])[0], "mla_kg": f(inp["ev_mla_k_gain"])[0],
            "od_qg": f(inp["od_q_gain"])[0], "od_kg": f(inp["od_k_gain"])[0], "subln": f(inp["od_subln"])[0],
            "memk0": f(inp["mem_k_gain"])[0], "memk1": f(inp["mem_k_gain"])[1],
            "sinks": f(inp["ev_sinks"])[0], "lam": f(inp["od_lambda"])[0].reshape(-1),
            "invf": np.asarray([10000.0 ** (-(i / 16.0)) for i in range(16)], np.float32)}
    grow = np.concatenate([rows[n] for n, _ in GROW]).astype(np.float32)
    cols = {"mix0": f(inp["mix_norm"])[0], "mix1": f(inp["mix_norm"])[1],
            "mqn0": f(inp["mem_q_norm"])[0], "mqn1": f(inp["mem_q_norm"])[1],
            "mkvn0": f(inp["mem_kv_norm"])[0], "mkvn1": f(inp["mem_kv_norm"])[1],
            "ffn0": f(inp["ffn_norm"])[0], "ffn1": f(inp["ffn_norm"])[1]}
    gcol = np.concatenate([cols[n].reshape(8, 128).T for n in GCOL], axis=1)
    mqg = np.stack([f(inp["mem_q_gain"])[0], f(inp["mem_q_gain"])[1]], axis=1)
    gcol = np.ascontiguousarray(np.concatenate([gcol, mqg], axis=1).astype(np.float32))
    return Wb, grow, gcol


def const_tables():
    c = {}
    c["ident"] = np.eye(128, dtype=np.float32).astype(bf)
    ik = np.arange(128)[:, None]
    iq = np.arange(128)[None, :]
    c["mcur"] = (iq >= ik).astype(np.float32).astype(bf)
    c["mprev"] = (iq < ik).astype(np.float32).astype(bf)
    qa = np.zeros((4, 8, 128), np.float32)
    for h in range(8):
        s = SLOPES[h]
        q = np.arange(128)
        qa[0, h] = -s * 16 * (q // 16)
        qa[1, h] = -s * (q % 16)
        qa[2, h] = 16 * s
        qa[3, h] = s
    c["swa_qaug"] = qa.astype(bf)
    ka = np.zeros((4, 128), np.float32)
    k = np.arange(128)
    ka[0] = 1
    ka[1] = 1
    ka[2] = k // 16
    ka[3] = k % 16
    c["swa_kaug"] = ka.astype(bf)
    off = np.zeros((1, 8, 128), np.float32)
    for h in range(8):
        off[0, h] = -128.0 * SLOPES[h]
    c["swa_off"] = off.astype(bf)
    da = np.zeros((8, 2, 512), np.float32)
    for h in range(8):
        s = SLOPES[h]
        q = np.arange(512)
        if h < 2:
            q = q % 128
        da[h, 0] = -s * 16 * (q // 16)
        da[h, 1] = -s * (q % 16)
    c["d_qaug"] = da.astype(bf)
    return c


def build(stage, S, dbg=False):
    NT = S // 512
    NL = NT // 4
    TOK = NL * 512
    NB = NL * 4
    nc = bass.Bass("TRN2", target_bir_lowering=False)
    R = Rec(nc)
    final = []

    def din(name, shape, dt=F32):
        return nc.dram_tensor(name, list(shape), dt, kind="ExternalInput").ap()

    def dout(name, shape, dt=F32):
        return nc.dram_tensor(name, list(shape), dt, kind="ExternalOutput").ap()

    def T(name, shape, dt=F32, psum=False):
        return Tile(nc, "t_" + name, shape, dt, psum)

    def dma(q, out_ap, in_ap, reads, writes, sem):
        R.op(q, lambda e: e.dma_start(out=out_ap, in_=in_ap), reads=reads, writes=writes, dma=sem)

    def load(q, t, src, extra_reads=()):
        dma(q, t[:], src, list(extra_reads), [t.b], t.b)

    wnames = STAGE_W[stage]
    w_in_ap = {n: din("W_" + n, [128, W_OFF[n][1]]) for n in wnames}
    wb = {n: nc.dram_tensor("WB_" + n, [128, W_OFF[n][1]], BF16).ap() for n in wnames}
    wb_bufs = {n: [Buf("wb_%s_%d" % (n, i)) for i in range((W_OFF[n][1] + CH - 1) // CH)] for n in wnames}
    wst = [T("wst%d" % i, [128, CH], F32) for i in range(2)]
    wbt = [T("wbt%d" % i, [128, CH], BF16) for i in range(3)]
    cnt = 0
    for n in wnames:
        sz = W_OFF[n][1]
        for i, o in enumerate(range(0, sz, CH)):
            c = min(CH, sz - o)
            st, bt = wst[cnt % 2], wbt[cnt % 3]
            dma("sp", st[:, 0:c], w_in_ap[n][:, o:o + c], [], [st.b], st.b)
            eng = ("dve", "pool", "act")[cnt % 3]
            if eng == "act":
                R.op("act", lambda e, st=st, bt=bt, c=c: e.copy(out=bt[:, 0:c], in_=st[:, 0:c]), reads=[st.b], writes=[bt.b])
            else:
                R.op(eng, lambda e, st=st, bt=bt, c=c: e.tensor_copy(out=bt[:, 0:c], in_=st[:, 0:c]), reads=[st.b], writes=[bt.b])
            dma("pool", wb[n][:, o:o + c], bt[:, 0:c], [bt.b], [wb_bufs[n][i]], bt.b)
            cnt += 1

    def wload(q, t_ap, tbuf, name, off, n):
        bufs = wb_bufs[name][off // CH:(off + n - 1) // CH + 1]
        dma(q, t_ap, wb[name][:, off:off + n], list(bufs), [tbuf], tbuf)

    def wtile(name, shape, pat=None):
        t = T("w_" + name, [128] + list(shape), BF16)
        n = int(np.prod(shape))
        flat = t[:] if len(shape) == 1 else t[:].rearrange(pat)
        wload("sp", flat, t.b, name, 0, n)
        return t

    ident = T("ident", [128, 128], BF16)
    load("sp", ident, din("c_ident", [128, 128], BF16))
    grow_d = din("grow", [GROW_TOTAL])
    grow = T("growt", [128, GROW_TOTAL], F32)
    load("sp", grow, grow_d.partition_broadcast(128))
    gcol = T("gcolt", [128, 66], F32)
    load("sp", gcol, din("gcol", [128, 66]))

    def GR(name):
        o, s = GROW_OFF[name]
        return grow[:, o:o + s]

    def GC(name):
        i = GCOL.index(name)
        return gcol[:, i * 8:(i + 1) * 8]

    epst = T("epst", [128, 1], F32)
    R.op("dve", lambda e: e.memset(epst[:], EPS), writes=[epst.b])
    ones_bf = T("ones_bf", [128, 128], BF16)
    R.op("dve", lambda e: e.memset(ones_bf[:], 1.0), writes=[ones_bf.b])

    TP = [T("TP%d" % i, [128, 1024], BF16, psum=True) for i in range(2)]
    PS = [T("PS%d" % i, [128, 512], F32, psum=True) for i in range(6)]

    junk = T("junk", [128, 1024], F32)
    xs = [T("xs%d" % i, [128, 1024], BF16) for i in range(2)]
    hT = T("hT", [128, 8, 640], BF16)
    ssb = T("ssb", [128, 8], F32)
    rsb = T("rsb", [128, 8], F32)

    def rstd_ops(out_ap, in_ap, n, obuf, ibuf, tmp):
        R.op("dve", lambda e: e.tensor_scalar(out=tmp[:, 0:in_ap.shape[1]], in0=in_ap, scalar1=1.0 / n, scalar2=None, op0=ALU.mult),
             reads=[ibuf], writes=[tmp.b])
        R.op("act", lambda e: e.activation(out=tmp[:, 0:in_ap.shape[1]], in_=tmp[:, 0:in_ap.shape[1]], func=AF.Sqrt, bias=epst[:, 0:1]),
             reads=[tmp.b, epst.b], writes=[tmp.b])
        R.op("dve", lambda e: e.reciprocal(out=out_ap, in_=tmp[:, 0:in_ap.shape[1]]), reads=[tmp.b], writes=[obuf])

    rtmp = T("rtmp", [128, 16], F32)
    tp_rr = [0]

    def rms_T(xt, blocks, gname, col0=0):
        nb_ = len(blocks)
        for i, blk in enumerate(blocks):
            R.op("act", lambda e, blk=blk, i=i: e.activation(out=junk[:], in_=xt[:, blk, :], func=AF.Square, accum_out=ssb[:, i:i + 1]),
                 reads=[xt.b], writes=[junk.b, ssb.b])
        rstd_ops(rsb[:, 0:nb_], ssb[:, 0:nb_], 1024.0, rsb.b, ssb.b, rtmp)
        gc = GC(gname)
        for i, blk in enumerate(blocks):
            x_ = xs[i % 2]
            R.op("act", lambda e, blk=blk, i=i, x_=x_: e.activation(out=x_[:], in_=xt[:, blk, :], func=AF.Copy, scale=rsb[:, i:i + 1]),
                 reads=[xt.b, rsb.b], writes=[x_.b])
            tp = TP[tp_rr[0] % 2]
            tp_rr[0] += 1
            for kc in range(8):
                R.op("pe", lambda e, kc=kc, x_=x_, tp=tp: e.transpose(out=tp[:, kc * 128:(kc + 1) * 128], in_=x_[:, kc * 128:(kc + 1) * 128], identity=ident[:]),
                     reads=[x_.b, ident.b], writes=[tp.b])
            c0 = (col0 + i) * 128
            R.op("dve", lambda e, tp=tp, c0=c0: e.tensor_tensor(out=hT[:, :, c0:c0 + 128], in0=tp[:, :].rearrange("p (k t) -> p k t", k=8),
                                                                 in1=gc.unsqueeze(2).to_broadcast([128, 8, 128]), op=ALU.mult),
                 reads=[gcol.b], writes=[tp.b, hT.b])

    def proj_tm(ps, ncols, c0h, w, wc0, nkc=8, lhs=None):
        for kc in range(nkc):
            R.op("pe", lambda e, kc=kc: e.matmul(ps[:, 0:ncols], lhsT=hT[:, kc, c0h:c0h + 128], rhs=w[:, kc, wc0:wc0 + ncols], start=(kc == 0), stop=(kc == nkc - 1)),
                 reads=[hT.b, w.b], writes=[ps.b])

    def bc(ap, axis, shape):
        return ap.unsqueeze(axis).to_broadcast(list(shape))

    PTs = [T("PT%d" % i, [128, 512], BF16) for i in range(4)]
    pt_rr = [0]

    x_in = None
    outputs = {}

    if stage == 1:
        x_d = din("x", [TOK, D])
        xh_d = din("xh", [NL * 128, D])
        pos_d = din("pos", [128, NB], I32)
        hv_d = din("hv", [128, NL])
        Oa_d = dout("Oa", [TOK, 512], BF16)
        Qm_d = dout("Qm", [NL, 8, 96, 512], BF16)
        Km_d = dout("Km", [NL * 8 * 96, 512], BF16)
        Vm_d = dout("Vm", [NL * 8 * 128, 4 * 65], BF16)
        w_in_t = wtile("w_in", [8, 1184], "p a b -> p (a b)")
        w_uq_t = wtile("w_uq", [2, 768], "p a b -> p (a b)")
        w_ukv_t = wtile("w_ukv", [1, 1024], "p a b -> p (a b)")
        mcur = T("mcur", [128, 128], BF16)
        load("sp", mcur, din("c_mcur", [128, 128], BF16))
        mprev = T("mprev", [128, 128], BF16)
        load("sp", mprev, din("c_mprev", [128, 128], BF16))
        hv = T("hv", [128, NL], F32)
        load("sp", hv, hv_d)
        onec = T("onec", [128, 1], F32)
        R.op("dve", lambda e: e.memset(onec[:], 1.0), writes=[onec.b])
        gq_s = T("gq_s", [128, 64], F32)
        R.op("dve", lambda e: e.tensor_scalar(out=gq_s[:], in0=GR("swa_qg"), scalar1=64.0 ** -0.5, scalar2=None, op0=ALU.mult), reads=[grow.b], writes=[gq_s.b])
        gmq_s = T("gmq_s", [128, 96], F32)
        R.op("dve", lambda e: e.tensor_scalar(out=gmq_s[:], in0=GR("mla_qg"), scalar1=96.0 ** -0.5, scalar2=None, op0=ALU.mult), reads=[grow.b], writes=[gmq_s.b])
        esink = T("esink", [128, 8], F32)
        R.op("act", lambda e: e.activation(out=esink[:], in_=GR("sinks"), func=AF.Exp), reads=[grow.b], writes=[esink.b])
        invn = T("invn", [128, 12], F32)
        R.op("dve", lambda e: e.memset(invn[:, 0:10], 1.0 / 64), writes=[invn.b])
        R.op("dve", lambda e: e.memset(invn[:, 10:11], 1.0 / 256), writes=[invn.b])
        R.op("dve", lambda e: e.memset(invn[:, 11:12], 1.0 / 128), writes=[invn.b])
        posi = T("posi", [128, NB], I32)
        load("sp", posi, pos_d)
        posf = T("posf", [128, NB], F32)
        R.op("dve", lambda e: e.tensor_copy(out=posf[:], in_=posi[:]), reads=[posi.b], writes=[posf.b])
        ang = T("ang", [128, NB, 16], F32)
        R.op("dve", lambda e: e.tensor_tensor(out=ang[:], in0=bc(posf[:, :], 2, [128, NB, 16]), in1=bc(GR("invf"), 1, [128, NB, 16]), op=ALU.mult),
             reads=[posf.b, grow.b], writes=[ang.b])
        SIN = T("SIN", [128, NB, 16], F32)
        COS = T("COS", [128, NB, 16], F32)
        ni = T("ni", [128, NB, 16], I32)
        nf = T("nf", [128, NB, 16], F32)
        rr_ = T("rr_", [128, NB, 16], F32)
        mm_ = T("mm_", [128, NB, 16], F32)
        TWO_PI = 2.0 * math.pi
        for (dst, shift) in ((SIN, 0.0), (COS, math.pi / 2)):
            R.op("dve", lambda e, shift=shift: e.tensor_scalar(out=nf[:], in0=ang[:], scalar1=shift, scalar2=1.0 / TWO_PI, op0=ALU.add, op1=ALU.mult), reads=[ang.b], writes=[nf.b])
            R.op("dve", lambda e: e.tensor_copy(out=ni[:], in_=nf[:]), reads=[nf.b], writes=[ni.b])
            R.op("dve", lambda e: e.tensor_copy(out=nf[:], in_=ni[:]), reads=[ni.b], writes=[nf.b])
            R.op("dve", lambda e, shift=shift: e.tensor_scalar(out=rr_[:], in0=ang[:], scalar1=shift, scalar2=None, op0=ALU.add), reads=[ang.b], writes=[rr_.b])
            R.op("dve", lambda e: e.scalar_tensor_tensor(out=rr_[:], in0=nf[:], scalar=-TWO_PI, in1=rr_[:], op0=ALU.mult, op1=ALU.add), reads=[nf.b, rr_.b], writes=[rr_.b])
            R.op("dve", lambda e: e.tensor_scalar(out=mm_[:], in0=rr_[:], scalar1=math.pi, scalar2=None, op0=ALU.is_gt), reads=[rr_.b], writes=[mm_.b])
            R.op("dve", lambda e: e.scalar_tensor_tensor(out=rr_[:], in0=mm_[:], scalar=-TWO_PI, in1=rr_[:], op0=ALU.mult, op1=ALU.add), reads=[mm_.b, rr_.b], writes=[rr_.b])
            R.op("dve", lambda e: e.tensor_scalar(out=mm_[:], in0=rr_[:], scalar1=-math.pi, scalar2=None, op0=ALU.is_lt), reads=[rr_.b], writes=[mm_.b])
            R.op("dve", lambda e: e.scalar_tensor_tensor(out=rr_[:], in0=mm_[:], scalar=TWO_PI, in1=rr_[:], op0=ALU.mult, op1=ALU.add), reads=[mm_.b, rr_.b], writes=[rr_.b])
            R.op("dve", lambda e: e.tensor_scalar(out=rr_[:], in0=rr_[:], scalar1=3.1415925, scalar2=-3.1415925, op0=ALU.min, op1=ALU.max), reads=[rr_.b], writes=[rr_.b])
            R.op("act", lambda e, dst=dst: e.activation(out=dst[:], in_=rr_[:], func=AF.Sin), reads=[rr_.b], writes=[dst.b])

        QaT = T("QaT", [68, 8, 128], BF16)
        KaT = T("KaT", [68, 2, 5, 128], BF16)
        Va = T("Va", [128, 5, 2, 65], BF16)
        offrow = T("offrow", [1, 8, 128], BF16)
        load("sp", offrow, din("c_swa_off", [1, 8, 128], BF16))
        dma("sp", QaT[64:68, :, :], din("c_swa_qaug", [4, 8, 128], BF16), [], [QaT.b], QaT.b)
        kaug_d = din("c_swa_kaug", [4, 128], BF16)
        for kvh in range(2):
            for blk in range(5):
                dma("sp", KaT[64:68, kvh, blk, :], kaug_d, [], [KaT.b], KaT.b)
        R.op("dve", lambda e: e.memset(Va[:], 1.0), writes=[Va.b])
        QmT = T("QmT", [96, 8, 512], BF16)
        KmT = T("KmT", [96, 8, 512], BF16)
        Vm = T("Vm_t", [128, 8, 4, 65], BF16)
        R.op("dve", lambda e: e.memset(Vm[:], 1.0), writes=[Vm.b])
        Oa_t = T("Oa_t", [128, 4, 512], BF16)
        xts = [T("xt%d" % i, [128, 5, 1024], F32) for i in range(2)]
        sq = T("sq", [128, 1024], F32)
        ssg = T("ssg", [128, 12], F32)
        rsg = T("rsg", [128, 12], F32)
        ssg2 = T("ssg2", [128, 16], F32)
        rs2 = T("rs2", [128, 16], F32)
        s_a = T("s_a", [128, 8], F32)
        s_b = T("s_b", [128, 8], F32)
        s_c = T("s_c", [128, 1], F32)
        tmpA = T("tmpA", [128, 512], F32)
        tmpB = T("tmpB", [128, 256], F32)
        qa_n = T("qa_n", [128, 512], BF16)
        ka_n = T("ka_n", [128, 128], BF16)
        cn = T("cn", [128, 384], BF16)
        cT = T("cT", [128, 3, 128], BF16)
        Qtm = T("Qtm", [128, 8, 96], BF16)
        Ktm = T("Ktm", [128, 8, 96], BF16)
        krg = T("krg", [128, 32], F32)
        krr = T("krr", [128, 32], F32)
        r_a = T("r_a", [128, 8, 16], F32)
        r_b = T("r_b", [128, 8, 16], F32)
        den = T("den", [128, 8], F32)
        PA, PB, PC, PD, PE_, PF = PS

        def rope(dst1, dst2, x1, x2, blk, nh, rbufs, wbuf):
            cs = bc(COS[:, blk, :], 1, [128, nh, 16]) if nh > 1 else COS[:, blk, :]
            sn = bc(SIN[:, blk, :], 1, [128, nh, 16]) if nh > 1 else SIN[:, blk, :]
            ra = r_a[:, 0:nh, :] if nh > 1 else r_a[:, 0, :]
            rb = r_b[:, 0:nh, :] if nh > 1 else r_b[:, 0, :]
            R.op("dve", lambda e: e.tensor_tensor(out=ra, in0=x1, in1=cs, op=ALU.mult), reads=rbufs + [COS.b], writes=[r_a.b])
            R.op("pool", lambda e: e.tensor_tensor(out=rb, in0=x2, in1=sn, op=ALU.mult), reads=rbufs + [SIN.b], writes=[r_b.b])
            R.op("dve", lambda e: e.tensor_tensor(out=dst1, in0=ra, in1=rb, op=ALU.subtract), reads=[r_a.b, r_b.b], writes=[wbuf])
            R.op("dve", lambda e: e.tensor_tensor(out=ra, in0=x2, in1=cs, op=ALU.mult), reads=rbufs + [COS.b], writes=[r_a.b])
            R.op("pool", lambda e: e.tensor_tensor(out=rb, in0=x1, in1=sn, op=ALU.mult), reads=rbufs + [SIN.b], writes=[r_b.b])
            R.op("dve", lambda e: e.tensor_tensor(out=dst2, in0=ra, in1=rb, op=ALU.add), reads=[r_a.b, r_b.b], writes=[wbuf])

        for lt in range(NL):
            xt = xts[lt % 2]
            dma("sp", xt[:, 0:4, :], x_d[lt * 512:(lt + 1) * 512, :].rearrange("(t p) c -> p t c", p=128), [], [xt.b], xt.b)
            dma("sp", xt[:, 4, :], xh_d[lt * 128:(lt + 1) * 128, :], [], [xt.b], xt.b)
            order = [4, 0, 1, 2, 3]
            rms_T(xt, order, "mix0")
            for i, blk in enumerate(order):
                halo = blk == 4
                c0h = i * 128
                kslot = 0 if halo else blk + 1
                gblk = lt * 4 + (0 if halo else blk)
                if not halo:
                    proj_tm(PA, 512, c0h, w_in_t, 0)
                    proj_tm(PB, 512, c0h, w_in_t, 512)
                    proj_tm(PC, 160, c0h, w_in_t, 1024)
                else:
                    proj_tm(PB, 256, c0h, w_in_t, 512)
                if not halo:
                    R.op("act", lambda e: e.activation(out=sq[:, 0:512], in_=PA[:, :], func=AF.Square), writes=[PA.b, sq.b])
                    R.op("dve", lambda e: e.tensor_reduce(out=ssg[:, 0:8], in_=sq[:, 0:512].rearrange("p (h d) -> p h d", h=8), axis=AX.X, op=ALU.add), reads=[sq.b], writes=[ssg.b])
                R.op("act", lambda e: e.activation(out=sq[:, 512:640], in_=PB[:, 0:128], func=AF.Square), writes=[PB.b, sq.b])
                R.op("dve", lambda e: e.tensor_reduce(out=ssg[:, 8:10], in_=sq[:, 512:640].rearrange("p (h d) -> p h d", h=2), axis=AX.X, op=ALU.add), reads=[sq.b], writes=[ssg.b])
                if not halo:
                    R.op("act", lambda e: e.activation(out=junk[:, 0:256], in_=PB[:, 256:512], func=AF.Square, accum_out=ssg[:, 10:11]), writes=[PB.b, junk.b, ssg.b])
                    R.op("act", lambda e: e.activation(out=junk[:, 0:128], in_=PC[:, 0:128], func=AF.Square, accum_out=ssg[:, 11:12]), writes=[PC.b, junk.b, ssg.b])
                lo, hi = (8, 10) if halo else (0, 12)
                R.op("dve", lambda e, lo=lo, hi=hi: e.tensor_tensor(out=rtmp[:, lo:hi], in0=ssg[:, lo:hi], in1=invn[:, lo:hi], op=ALU.mult), reads=[ssg.b, invn.b], writes=[rtmp.b])
                R.op("act", lambda e, lo=lo, hi=hi: e.activation(out=rtmp[:, lo:hi], in_=rtmp[:, lo:hi], func=AF.Sqrt, bias=epst[:, 0:1]), reads=[rtmp.b, epst.b], writes=[rtmp.b])
                R.op("dve", lambda e, lo=lo, hi=hi: e.reciprocal(out=rsg[:, lo:hi], in_=rtmp[:, lo:hi]), reads=[rtmp.b], writes=[rsg.b])
                R.op("dve", lambda e: e.tensor_tensor(out=tmpB[:, 0:128].rearrange("p (h d) -> p h d", h=2), in0=PB[:, 0:128].rearrange("p (h d) -> p h d", h=2),
                                                       in1=bc(rsg[:, 8:10], 2, [128, 2, 64]), op=ALU.mult), reads=[rsg.b], writes=[PB.b, tmpB.b])
                R.op("pool", lambda e: e.tensor_tensor(out=ka_n[:, :].rearrange("p (h d) -> p h d", h=2), in0=tmpB[:, 0:128].rearrange("p (h d) -> p h d", h=2),
                                                        in1=bc(GR("swa_kg"), 1, [128, 2, 64]), op=ALU.mult), reads=[tmpB.b, grow.b], writes=[ka_n.b])
                R.op("act", lambda e, kslot=kslot: e.copy(out=Va[:, kslot, :, 0:64], in_=PB[:, 128:256].rearrange("p (h d) -> p h d", h=2)), writes=[PB.b, Va.b])
                tpb = TP[1]
                for kvh in range(2):
                    R.op("pe", lambda e, kvh=kvh: e.transpose(out=tpb[0:64, kvh * 128:(kvh + 1) * 128], in_=ka_n[:, kvh * 64:(kvh + 1) * 64], identity=ident[:]),
                         reads=[ka_n.b, ident.b], writes=[tpb.b])
                if halo:
                    R.op("dve", lambda e, kslot=kslot: e.tensor_copy(out=KaT[0:64, :, kslot, :], in_=tpb[0:64, 0:256].rearrange("p (h t) -> p h t", h=2)), writes=[tpb.b, KaT.b])
                    continue
                R.op("dve", lambda e: e.tensor_tensor(out=tmpA[:, :].rearrange("p (h d) -> p h d", h=8), in0=PA[:, :].rearrange("p (h d) -> p h d", h=8),
                                                       in1=bc(rsg[:, 0:8], 2, [128, 8, 64]), op=ALU.mult), reads=[rsg.b], writes=[PA.b, tmpA.b])
                R.op("pool", lambda e: e.tensor_tensor(out=qa_n[:, :].rearrange("p (h d) -> p h d", h=8), in0=tmpA[:, :].rearrange("p (h d) -> p h d", h=8),
                                                        in1=bc(gq_s[:, :], 1, [128, 8, 64]), op=ALU.mult), reads=[tmpA.b, gq_s.b], writes=[qa_n.b])
                R.op("dve", lambda e: e.scalar_tensor_tensor(out=cn[:, 0:256], in0=PB[:, 256:512], scalar=rsg[:, 10:11], in1=GR("qlat"), op0=ALU.mult, op1=ALU.mult),
                     reads=[rsg.b, grow.b], writes=[PB.b, cn.b])
                R.op("dve", lambda e: e.scalar_tensor_tensor(out=cn[:, 256:384], in0=PC[:, 0:128], scalar=rsg[:, 11:12], in1=GR("kvlat"), op0=ALU.mult, op1=ALU.mult),
                     reads=[rsg.b, grow.b], writes=[PC.b, cn.b])
                R.op("dve", lambda e: e.tensor_tensor(out=krg[:], in0=PC[:, 128:160], in1=GR("mla_kg")[:, 64:96], op=ALU.mult), reads=[grow.b], writes=[PC.b, krg.b])
                R.op("act", lambda e: e.activation(out=junk[:, 0:32], in_=PC[:, 128:160], func=AF.Square, accum_out=s_c[:, 0:1]), writes=[PC.b, junk.b, s_c.b])
                tpa = TP[0]
                for h in range(8):
                    R.op("pe", lambda e, h=h: e.transpose(out=tpa[0:64, h * 128:(h + 1) * 128], in_=qa_n[:, h * 64:(h + 1) * 64], identity=ident[:]),
                         reads=[qa_n.b, ident.b], writes=[tpa.b])
                for j in range(3):
                    R.op("pe", lambda e, j=j: e.transpose(out=tpb[:, 256 + j * 128:256 + (j + 1) * 128], in_=cn[:, j * 128:(j + 1) * 128], identity=ident[:]),
                         reads=[cn.b, ident.b], writes=[tpb.b])
                R.op("act", lambda e: e.copy(out=QaT[0:64, :, :], in_=tpa[0:64, :].rearrange("p (h t) -> p h t", h=8)), writes=[tpa.b, QaT.b])
                R.op("dve", lambda e, kslot=kslot: e.tensor_copy(out=KaT[0:64, :, kslot, :], in_=tpb[0:64, 0:256].rearrange("p (h t) -> p h t", h=2)), writes=[tpb.b, KaT.b])
                R.op("dve", lambda e: e.tensor_copy(out=cT[:, :, :], in_=tpb[:, 256:640].rearrange("p (j t) -> p j t", j=3)), writes=[tpb.b, cT.b])
                for kc in range(2):
                    R.op("pe", lambda e, kc=kc: e.matmul(PD[:, 0:512], lhsT=cT[:, kc, :], rhs=w_uq_t[:, kc, 0:512], start=(kc == 0), stop=(kc == 1)), reads=[cT.b, w_uq_t.b], writes=[PD.b])
                for kc in range(2):
                    R.op("pe", lambda e, kc=kc: e.matmul(PE_[:, 0:256], lhsT=cT[:, kc, :], rhs=w_uq_t[:, kc, 512:768], start=(kc == 0), stop=(kc == 1)), reads=[cT.b, w_uq_t.b], writes=[PE_.b])
                R.op("pe", lambda e: e.matmul(PA[:, 0:512], lhsT=cT[:, 2, :], rhs=w_ukv_t[:, 0, 0:512], start=True, stop=True), reads=[cT.b, w_ukv_t.b], writes=[PA.b])
                R.op("pe", lambda e: e.matmul(PB[:, 0:512], lhsT=cT[:, 2, :], rhs=w_ukv_t[:, 0, 512:1024], start=True, stop=True), reads=[cT.b, w_ukv_t.b], writes=[PB.b])
                R.op("act", lambda e: e.activation(out=sq[:, 0:512], in_=PD[:, :], func=AF.Square), writes=[PD.b, sq.b])
                R.op("dve", lambda e: e.tensor_reduce(out=s_a[:], in_=sq[:, 0:512].rearrange("p (h d) -> p h d", h=8), axis=AX.X, op=ALU.add), reads=[sq.b], writes=[s_a.b])
                R.op("act", lambda e: e.activation(out=sq[:, 512:768], in_=PE_[:, 0:256], func=AF.Square), writes=[PE_.b, sq.b])
                R.op("dve", lambda e: e.tensor_reduce(out=s_b[:], in_=sq[:, 512:768].rearrange("p (h d) -> p h d", h=8), axis=AX.X, op=ALU.add), reads=[sq.b], writes=[s_b.b])
                R.op("dve", lambda e: e.tensor_tensor(out=ssg2[:, 0:8], in0=s_a[:], in1=s_b[:], op=ALU.add), reads=[s_a.b, s_b.b], writes=[ssg2.b])
                R.op("act", lambda e: e.activation(out=sq[:, 0:512], in_=PA[:, :], func=AF.Square), writes=[PA.b, sq.b])
                R.op("dve", lambda e: e.tensor_reduce(out=s_a[:], in_=sq[:, 0:512].rearrange("p (h d) -> p h d", h=8), axis=AX.X, op=ALU.add), reads=[sq.b], writes=[s_a.b])
                R.op("dve", lambda e: e.tensor_scalar(out=ssg2[:, 8:16], in0=s_a[:], scalar1=s_c[:, 0:1], scalar2=None, op0=ALU.add), reads=[s_a.b, s_c.b], writes=[ssg2.b])
                rstd_ops(rs2[:, 0:16], ssg2[:, 0:16], 96.0, rs2.b, ssg2.b, rtmp)
                R.op("dve", lambda e: e.tensor_tensor(out=tmpA[:, :].rearrange("p (h d) -> p h d", h=8), in0=PD[:, :].rearrange("p (h d) -> p h d", h=8),
                                                       in1=bc(rs2[:, 0:8], 2, [128, 8, 64]), op=ALU.mult), reads=[rs2.b], writes=[PD.b, tmpA.b])
                R.op("pool", lambda e: e.tensor_tensor(out=Qtm[:, :, 0:64], in0=tmpA[:, :].rearrange("p (h d) -> p h d", h=8),
                                                        in1=bc(gmq_s[:, 0:64], 1, [128, 8, 64]), op=ALU.mult), reads=[tmpA.b, gmq_s.b], writes=[Qtm.b])
                R.op("dve", lambda e: e.tensor_tensor(out=tmpB[:, :].rearrange("p (h d) -> p h d", h=8), in0=PE_[:, 0:256].rearrange("p (h d) -> p h d", h=8),
                                                       in1=bc(rs2[:, 0:8], 2, [128, 8, 32]), op=ALU.mult), reads=[rs2.b], writes=[PE_.b, tmpB.b])
                R.op("pool", lambda e: e.tensor_tensor(out=tmpB[:, :].rearrange("p (h d) -> p h d", h=8), in0=tmpB[:, :].rearrange("p (h d) -> p h d", h=8),
                                                        in1=bc(gmq_s[:, 64:96], 1, [128, 8, 32]), op=ALU.mult), reads=[tmpB.b, gmq_s.b], writes=[tmpB.b])
                tB3 = tmpB[:, :].rearrange("p (h d) -> p h d", h=8)
                rope(Qtm[:, :, 64:80], Qtm[:, :, 80:96], tB3[:, :, 0:16], tB3[:, :, 16:32], gblk, 8, [tmpB.b], Qtm.b)
                R.op("dve", lambda e: e.tensor_tensor(out=tmpA[:, :].rearrange("p (h d) -> p h d", h=8), in0=PA[:, :].rearrange("p (h d) -> p h d", h=8),
                                                       in1=bc(rs2[:, 8:16], 2, [128, 8, 64]), op=ALU.mult), reads=[rs2.b], writes=[PA.b, tmpA.b])
                R.op("pool", lambda e: e.tensor_tensor(out=Ktm[:, :, 0:64], in0=tmpA[:, :].rearrange("p (h d) -> p h d", h=8),
                                                        in1=bc(GR("mla_kg")[:, 0:64], 1, [128, 8, 64]), op=ALU.mult), reads=[tmpA.b, grow.b], writes=[Ktm.b])
                rope(krr[:, 0:16], krr[:, 16:32], krg[:, 0:16], krg[:, 16:32], gblk, 1, [krg.b], krr.b)
                R.op("dve", lambda e: e.tensor_tensor(out=Ktm[:, :, 64:96], in0=bc(krr[:, :], 1, [128, 8, 32]), in1=bc(rs2[:, 8:16], 2, [128, 8, 32]), op=ALU.mult),
                     reads=[krr.b, rs2.b], writes=[Ktm.b])
                R.op("act", lambda e, blk=blk: e.copy(out=Vm[:, :, blk, 0:64], in_=PB[:, :].rearrange("p (h d) -> p h d", h=8)), writes=[PB.b, Vm.b])
                for h in range(8):
                    R.op("pe", lambda e, h=h: e.transpose(out=tpa[0:96, h * 128:(h + 1) * 128], in_=Qtm[:, h, :], identity=ident[:]), reads=[Qtm.b, ident.b], writes=[tpa.b])
                for h in range(8):
                    R.op("pe", lambda e, h=h: e.transpose(out=tpb[0:96, h * 128:(h + 1) * 128], in_=Ktm[:, h, :], identity=ident[:]), reads=[Ktm.b, ident.b], writes=[tpb.b])
                R.op("act", lambda e, blk=blk: e.copy(out=QmT[:, :, blk * 128:(blk + 1) * 128], in_=tpa[0:96, :].rearrange("p (h t) -> p h t", h=8)), writes=[tpa.b, QmT.b])
                R.op("dve", lambda e, blk=blk: e.tensor_copy(out=KmT[:, :, blk * 128:(blk + 1) * 128], in_=tpb[0:96, :].rearrange("p (h t) -> p h t", h=8)), writes=[tpb.b, KmT.b])
                for kvh in range(2):
                    ACC = PD if kvh == 0 else PE_
                    first = True
                    for role in (0, 1):
                        ks = kslot - 1 if role == 0 else kslot
                        R.op("pe", lambda e, kvh=kvh, ks=ks, role=role: e.matmul(PF[:, :], lhsT=KaT[0:68, kvh, ks, :], rhs=QaT[0:68, kvh * 4:(kvh + 1) * 4, :],
                                                                                  start=True, stop=(role == 1)), reads=[KaT.b, QaT.b], writes=[PF.b])
                        if role == 0:
                            R.op("pe", lambda e, kvh=kvh: e.matmul(PF[:, :], lhsT=ones_bf[0:1, 0:128], rhs=offrow[0:1, kvh * 4:(kvh + 1) * 4, :], start=False, stop=True),
                                 reads=[ones_bf.b, offrow.b], writes=[PF.b])
                        pt = PTs[pt_rr[0] % 4]
                        pt_rr[0] += 1
                        if dbg and lt == 0 and blk == 1:
                            dS = T("dS%d%d" % (kvh, role), [128, 512], F32)
                            R.op("dve", lambda e, dS=dS: e.tensor_copy(out=dS[:], in_=PF[:, :]), writes=[PF.b, dS.b])
                            dma("pool", dout("dbgS%d%d" % (kvh, role), [128, 512]), dS[:], [dS.b], [Buf("x")], dS.b)
                            final.append(dS.b)
                        R.op("act", lambda e, pt=pt: e.activation(out=pt[:], in_=PF[:, :], func=AF.Exp), writes=[PF.b, pt.b])
                        p3 = pt[:, :].rearrange("p (h t) -> p h t", h=4)
                        if role == 0:
                            sc = hv[:, lt:lt + 1] if blk == 0 else onec[:, 0:1]
                            R.op("dve", lambda e, p3=p3, sc=sc: e.scalar_tensor_tensor(out=p3, in0=p3, scalar=sc, in1=bc(mprev[:, :], 1, [128, 4, 128]), op0=ALU.mult, op1=ALU.mult),
                                 reads=[pt.b, mprev.b, hv.b, onec.b], writes=[pt.b])
                        else:
                            R.op("dve", lambda e, p3=p3: e.tensor_tensor(out=p3, in0=p3, in1=bc(mcur[:, :], 1, [128, 4, 128]), op=ALU.mult), reads=[pt.b, mcur.b], writes=[pt.b])
                        for hh in range(4):
                            R.op("pe", lambda e, hh=hh, pt=pt, kvh=kvh, ks=ks, first=first, role=role: e.matmul(
                                ACC[:, hh * 65:(hh + 1) * 65], lhsT=pt[:, hh * 128:(hh + 1) * 128], rhs=Va[:, ks, kvh, :], start=first, stop=(role == 1), skip_group_check=True),
                                reads=[pt.b, Va.b], writes=[ACC.b])
                            first = False
                    a3 = ACC[:, 0:260].rearrange("p (h c) -> p h c", h=4)
                    if dbg and lt == 0 and blk == 1:
                        dA = T("dA%d" % kvh, [128, 260], F32)
                        R.op("dve", lambda e, dA=dA, ACC=ACC: e.tensor_copy(out=dA[:], in_=ACC[:, 0:260]), writes=[ACC.b, dA.b])
                        dma("pool", dout("dbgA%d" % kvh, [128, 260]), dA[:], [dA.b], [Buf("x")], dA.b)
                        final.append(dA.b)
                    R.op("dve", lambda e, a3=a3, kvh=kvh: e.tensor_tensor(out=den[:, kvh * 4:(kvh + 1) * 4], in0=a3[:, :, 64], in1=esink[:, kvh * 4:(kvh + 1) * 4], op=ALU.add),
                         reads=[esink.b], writes=[ACC.b, den.b])
                    R.op("dve", lambda e, kvh=kvh: e.reciprocal(out=den[:, kvh * 4:(kvh + 1) * 4], in_=den[:, kvh * 4:(kvh + 1) * 4]), reads=[den.b], writes=[den.b])
                    R.op("dve", lambda e, a3=a3, kvh=kvh, blk=blk: e.tensor_tensor(out=Oa_t[:, blk, kvh * 256:(kvh + 1) * 256].rearrange("p (h d) -> p h d", h=4), in0=a3[:, :, 0:64],
                                                                                    in1=bc(den[:, kvh * 4:(kvh + 1) * 4], 2, [128, 4, 64]), op=ALU.mult),
                         reads=[den.b], writes=[ACC.b, Oa_t.b])
            ob = Buf("o")
            dma("pool", Oa_d[lt * 512:(lt + 1) * 512, :].rearrange("(t p) c -> p t c", p=128), Oa_t[:], [Oa_t.b], [ob], Oa_t.b)
            dma("pool", Qm_d[lt].rearrange("h d t -> d h t"), QmT[:], [QmT.b], [ob], QmT.b)
            dma("pool", Km_d[lt * 768:(lt + 1) * 768, :].rearrange("(h d) t -> d h t", h=8), KmT[:], [KmT.b], [ob], KmT.b)
            dma("pool", Vm_d[lt * 1024:(lt + 1) * 1024, :].rearrange("(h p) c -> p h c", h=8), Vm[:].rearrange("p h k c -> p h (k c)"), [Vm.b], [ob], Vm.b)
            final.extend([Oa_t.b, QmT.b, KmT.b, Vm.b])

    if stage in (2, 3):
        L = stage - 2
        x_d = din("x", [TOK, D])
        mem_d = din("mem", [MEMT, D])
        qidx_d = din("qidx", [TOK])
        kidx_d = din("kidx", [128, NT * 4])
        q0_d = din("q0", [128, NL])
        xo_d = dout("xo", [TOK, D])
        kidx = T("kidx", [128, NT * 4], F32)
        load("sp", kidx, kidx_d)
        q0t = T("q0t", [128, NL], F32)
        load("sp", q0t, q0_d)
        if stage == 2:
            Oa_d = din("Oa", [TOK, 512], BF16)
            Q_d = din("Qm", [NL, 8, 96, 512], BF16)
            K_d = din("Kall", [4 * NL * 8 * 96, 512], BF16)
            V_d = din("Vall", [4 * NL * 8 * 128, 4 * 65], BF16)
            Qd_o = dout("Qd", [NL, 8, 2, 64, 512], BF16)
            Kd_o = dout("Kd", [NL * 8 * 2 * 64, 512], BF16)
            Vd_o = dout("Vd", [NL * 8 * 128, 4 * 129], BF16)
            WO = "ev_wo"
            DK, DV, NCOMP = 96, 64, 1
        else:
            Q_d = din("Qd", [NL, 8, 2, 64, 512], BF16)
            K_d = din("Kall", [4 * NL * 8 * 2 * 64, 512], BF16)
            V_d = din("Vall", [4 * NL * 8 * 128, 4 * 129], BF16)
            qaug_d = din("c_d_qaug", [8, 2, 512], BF16)
            WO = "od_wo"
            DK, DV, NCOMP = 66, 128, 2
        sfx = "%d" % L
        S0, S1, A0, A1, A2, A3 = PS
        ACCS = [A0, A1, A2, A3]
        wchs = [T("wch%d" % i, [128, 8, 512], BF16) for i in range(3)]
        wch_rr = [0]

        def wstream(name, kdim, ntot, c0):
            t = wchs[wch_rr[0] % 3]
            wch_rr[0] += 1
            src = wb[name].rearrange("p (k n) -> p k n", k=kdim)[:, :, c0:c0 + 512]
            lo = c0
            hi = (kdim - 1) * ntot + c0 + 512
            bufs = wb_bufs[name][lo // CH:(hi - 1) // CH + 1]
            dma("sp", t[:, 0:kdim, :], src, list(bufs), [t.b], t.b)
            return t

        xt = T("xt", [128, 4, 1024], F32)
        dma("sp", xt[:, 0:2, :], mem_d.rearrange("(t p) c -> p t c", p=128), [], [xt.b], xt.b)
        rms_T(xt, [0, 1], "mkvn" + sfx)
        KmemT = T("KmemT", [128, 4, 256], BF16)
        Vmem = T("Vmem", [128, 2, 4, 129], BF16)
        R.op("dve", lambda e: e.memset(Vmem[:], 1.0), writes=[Vmem.b])
        sqm = T("sqm", [128, 512], F32)
        ssm = T("ssm", [128, 4], F32)
        rsm = T("rsm", [128, 4], F32)
        tmpM = T("tmpM", [128, 512], F32)
        kmn = T("kmn", [128, 512], BF16)
        wk_ = wstream("mkv" + sfx, 8, 1024, 0)
        wv_ = wstream("mkv" + sfx, 8, 1024, 512)
        for mb in range(2):
            proj_tm(S0, 512, mb * 128, wk_, 0)
            proj_tm(S1, 512, mb * 128, wv_, 0)
            R.op("act", lambda e: e.activation(out=sqm[:], in_=S0[:, :], func=AF.Square), writes=[S0.b, sqm.b])
            R.op("dve", lambda e: e.tensor_reduce(out=ssm[:], in_=sqm[:, :].rearrange("p (h d) -> p h d", h=4), axis=AX.X, op=ALU.add), reads=[sqm.b], writes=[ssm.b])
            rstd_ops(rsm[:, 0:4], ssm[:, 0:4], 128.0, rsm.b, ssm.b, rtmp)
            R.op("dve", lambda e: e.tensor_tensor(out=tmpM[:, :].rearrange("p (h d) -> p h d", h=4), in0=S0[:, :].rearrange("p (h d) -> p h d", h=4),
                                                   in1=bc(rsm[:, 0:4], 2, [128, 4, 128]), op=ALU.mult), reads=[rsm.b], writes=[S0.b, tmpM.b])
            R.op("pool", lambda e: e.tensor_tensor(out=kmn[:, :].rearrange("p (h d) -> p h d", h=4), in0=tmpM[:, :].rearrange("p (h d) -> p h d", h=4),
                                                    in1=bc(GR("memk" + sfx), 1, [128, 4, 128]), op=ALU.mult), reads=[tmpM.b, grow.b], writes=[kmn.b])
            R.op("act", lambda e, mb=mb: e.copy(out=Vmem[:, mb, :, 0:128], in_=S1[:, :].rearrange("p (h d) -> p h d", h=4)), writes=[S1.b, Vmem.b])
            tp = TP[0]
            for h in range(4):
                R.op("pe", lambda e, h=h: e.transpose(out=tp[:, h * 128:(h + 1) * 128], in_=kmn[:, h * 128:(h + 1) * 128], identity=ident[:]), reads=[kmn.b, ident.b], writes=[tp.b])
            R.op("dve", lambda e, mb=mb: e.tensor_copy(out=KmemT[:, :, mb * 128:(mb + 1) * 128], in_=tp[:, 0:512].rearrange("p (h t) -> p h t", h=4)), writes=[tp.b, KmemT.b])
        gmq_col = T("gmq_col", [128, 1], F32)
        R.op("dve", lambda e: e.tensor_scalar(out=gmq_col[:], in0=gcol[:, 64 + L:65 + L], scalar1=128.0 ** -0.5, scalar2=None, op0=ALU.mult), reads=[gcol.b], writes=[gmq_col.b])

        Ot = T("Ot", [128, 4, 1024], BF16)
        OT = T("OT", [128, 8, 512], BF16)
        AT = T("AT", [128, NFC, 512], BF16)
        qidxb = T("qidxb", [128, 512], F32)
        QTs = [T("QT%d" % i, [DK if stage == 2 else 66, NCOMP, 512], BF16) for i in range(2)]
        NKV = 5
        KTs = [T("KT%d" % i, [DK if stage == 2 else 66, NCOMP, 512], BF16) for i in range(NKV)]
        VTs = [T("VT%d" % i, [128, 4, DV + 1], BF16) for i in range(NKV)]
        if stage == 3:
            for kt in KTs:
                R.op("dve", lambda e, kt=kt: e.memset(kt[64:66, :, :], 1.0), writes=[kt.b])
        rec4 = T("rec4", [128, 8], F32)
        ball = T("ball", [128, 4, NT * 4], F32)
        gus = [T("gu%d" % i, [128, 2, 8, 128], BF16) for i in range(2)]
        wds = [T("wd%d" % i, [128, 2, 512], BF16) for i in range(3)]
        sgs = [T("sg%d" % i, [128, 512], F32) for i in range(2)]
        qTm = T("qTm", [128, 4, 512], BF16)
        sqb = T("sqb", [128, 512], BF16)
        rstf = T("rstf", [128, 512], F32)
        Om = T("Om", [128, 4, 512], BF16)
        OmT = T("OmT", [128, 4, 512], BF16)
        kv_rr = [0]
        q_rr = [0]
        acc_rr = [0]
        if stage == 3:
            lamt = T("lamt", [128, 4], F32)
            lt1 = T("lt1", [128, 128], F32)
            lam_ap = GR("lam")
            R.op("dve", lambda e: e.tensor_tensor(out=lt1[:, 0:64], in0=lam_ap[:, 0:64], in1=lam_ap[:, 64:128], op=ALU.mult), reads=[grow.b], writes=[lt1.b])
            R.op("dve", lambda e: e.tensor_tensor(out=lt1[:, 64:128], in0=lam_ap[:, 128:192], in1=lam_ap[:, 192:256], op=ALU.mult), reads=[grow.b], writes=[lt1.b])
            R.op("dve", lambda e: e.tensor_reduce(out=lamt[:, 0:2], in_=lt1[:, :].rearrange("p (a d) -> p a d", a=2), axis=AX.X, op=ALU.add), reads=[lt1.b], writes=[lamt.b])
            R.op("act", lambda e: e.activation(out=lamt[:, 0:2], in_=lamt[:, 0:2], func=AF.Exp), reads=[lamt.b], writes=[lamt.b])
            R.op("dve", lambda e: e.tensor_tensor(out=lamt[:, 2:3], in0=lamt[:, 1:2], in1=lamt[:, 0:1], op=ALU.subtract), reads=[lamt.b], writes=[lamt.b])
            R.op("dve", lambda e: e.tensor_scalar(out=lamt[:, 3:4], in0=lamt[:, 2:3], scalar1=-LAMBDA_INIT, scalar2=None, op0=ALU.add), reads=[lamt.b], writes=[lamt.b])
            gsub = T("gsub", [128, 128], F32)
            R.op("dve", lambda e: e.tensor_scalar(out=gsub[:], in0=GR("subln"), scalar1=(1.0 - LAMBDA_INIT), scalar2=None, op0=ALU.mult), reads=[grow.b], writes=[gsub.b])
            dd = T("dd", [128, 4, 128], F32)
            t1 = T("t1", [128, 128], F32)
            rr2 = T("rr2", [128, 2], F32)
            ssd = T("ssd", [128, 4], F32)
            rsd = T("rsd", [128, 4], F32)
        else:
            gq_s = T("gq_s", [128, 64], F32)
            R.op("dve", lambda e: e.tensor_scalar(out=gq_s[:], in0=GR("od_qg"), scalar1=64.0 ** -0.5, scalar2=None, op0=ALU.mult), reads=[grow.b], writes=[gq_s.b])
            QKst = [T("QKst%d" % i, [64, 8, 512], BF16) for i in range(2)]
            Vst = [T("Vst%d" % i, [128, 4, 4, 129], BF16) for i in range(2)]
            for v_ in Vst:
                R.op("dve", lambda e, v_=v_: e.memset(v_[:], 1.0), writes=[v_.b])
            sqq = T("sqq", [128, 512], F32)
            ssq = T("ssq", [128, 8], F32)
            rsq = T("rsq", [128, 8], F32)
            tmpQ = T("tmpQ", [128, 512], F32)
            Qtm2 = T("Qtm2", [128, 512], BF16)

        Kall_b = Buf("Kall")
        Vall_b = Buf("Vall")

        def attn_steps(QT, steps, nq, fin_cb):
            pend = None
            nsteps = len(steps)
            loaded = [0]

            def ensure(upto):
                while loaded[0] < min(upto, nsteps):
                    lf = steps[loaded[0]].get("load")
                    if lf is not None:
                        lf()
                    loaded[0] += 1
            for si, st in enumerate(steps + [None]):
                cur = None
                if st is not None:
                    ensure(si + 13)
                    pts = []
                    for c in range(NCOMP if st["kind"] == "dense" else 1):
                        Sb = (S0, S1)[(st["sidx"] + c) % 2]
                        R.op("pe", lambda e, st=st, c=c, Sb=Sb: e.matmul(Sb[:, 0:nq], lhsT=st["kt"][c], rhs=st["qt"][c], start=True, stop=True), reads=st["rd_qk"], writes=[Sb.b])
                        pt = PTs[pt_rr[0] % 4]
                        pt_rr[0] += 1
                        if st.get("steep"):
                            for qb in range(4):
                                R.op("act", lambda e, pt=pt, Sb=Sb, qb=qb, st=st: e.activation(out=pt[:, qb * 128:(qb + 1) * 128], in_=Sb[:, qb * 128:(qb + 1) * 128], func=AF.Exp,
                                                                                                bias=st["bias"][qb]), reads=st["rd_b"], writes=[Sb.b, pt.b])
                        elif st.get("bias") is not None:
                            R.op("act", lambda e, pt=pt, Sb=Sb, st=st: e.activation(out=pt[:, 0:nq], in_=Sb[:, 0:nq], func=AF.Exp, bias=st["bias"]), reads=st["rd_b"], writes=[Sb.b, pt.b])
                        else:
                            R.op("act", lambda e, pt=pt, Sb=Sb: e.activation(out=pt[:, 0:nq], in_=Sb[:, 0:nq], func=AF.Exp), writes=[Sb.b, pt.b])
                        if st.get("mask") is not None:
                            R.op("dve", lambda e, pt=pt, st=st: e.scalar_tensor_tensor(out=pt[:, 0:nq], in0=qidxb[:, 0:nq], scalar=st["mask"], in1=pt[:, 0:nq], op0=ALU.is_ge, op1=ALU.mult),
                                 reads=[qidxb.b, kidx.b, pt.b], writes=[pt.b])
                        pts.append(pt)
                    cur = (st, pts)
                if pend is not None:
                    pst, ppts = pend
                    for c, pt in enumerate(ppts):
                        for qb in range(nq // 128):
                            acc_ap, accb, startf = pst["acc"](c, qb)
                            R.op("pe", lambda e, pt=pt, qb=qb, acc_ap=acc_ap, startf=startf, pst=pst: e.matmul(acc_ap, lhsT=pt[:, qb * 128:(qb + 1) * 128], rhs=pst["v"], start=startf,
                                                                                                       stop=pst["last"], skip_group_check=True), reads=[pt.b] + pst["rd_v"], writes=[accb])
                    if pst["last"]:
                        fin_cb(pst)
                pend = cur

        for lt in range(NL):
            p = lt // 2
            gmin = 8 * p if lt % 2 == 0 else 8 * p + 4
            gmax = gmin + 3
            dma("sp", xt[:], x_d[lt * 512:(lt + 1) * 512, :].rearrange("(t p) c -> p t c", p=128), [], [xt.b], xt.b)
            dma("sp", qidxb[:], qidx_d[lt * 512:(lt + 1) * 512].partition_broadcast(128), [], [qidxb.b], qidxb.b)
            if stage == 2:
                dma("sp", Ot[:, :, 0:512], Oa_d[lt * 512:(lt + 1) * 512, :].rearrange("(t p) c -> p t c", p=128), [], [Ot.b], Ot.b)
            for h in range(8):
                QT = QTs[q_rr[0] % 2]
                q_rr[0] += 1
                if stage == 2:
                    dma("sp", QT[:, 0, :], Q_d[lt, h], [], [QT.b], QT.b)
                else:
                    dma("sp", QT[0:64, :, :], Q_d[lt, h].rearrange("c d t -> d c t"), [], [QT.b], QT.b)
                    for c in range(2):
                        dma("sp", QT[64:66, c, :], qaug_d[h], [], [QT.b], QT.b)
                steep = (stage == 3 and h < 2)
                if stage == 3:
                    s = SLOPES[h]
                    nqb = 4 if steep else 1
                    for qb in range(nqb):
                        R.op("dve", lambda e, qb=qb, s=s: e.tensor_scalar(out=ball[:, qb, :], in0=kidx[:, :], scalar1=q0t[:, lt:lt + 1], scalar2=s, op0=ALU.subtract, op1=ALU.mult),
                             reads=[kidx.b, q0t.b], writes=[ball.b])
                        cap = s * (127.0 if steep else 511.0)
                        R.op("dve", lambda e, qb=qb, s=s, cap=cap: e.tensor_scalar(out=ball[:, qb, :], in0=ball[:, qb, :], scalar1=-128.0 * qb * s, scalar2=cap, op0=ALU.add, op1=ALU.min),
                             reads=[ball.b], writes=[ball.b])
                if stage == 2:
                    ACC = ACCS[acc_rr[0] % 4]
                    acc_rr[0] += 1
                steps = []
                glist = []
                for g in range(0, gmax + 1):
                    if stage == 3 and g < gmin - 1:
                        dmin = (gmin - g - 1) * 512 + 1
                        if SLOPES[h] * dmin >= 110.0:
                            continue
                    glist.append(g)
                started = {}
                for gi, g in enumerate(glist):
                    jj, ltk = owner(g)
                    slot = jj * NL + ltk
                    KT = KTs[kv_rr[0] % NKV]
                    VT = VTs[kv_rr[0] % NKV]
                    kv_rr[0] += 1
                    if stage == 2:
                        def loadf(KT=KT, VT=VT, slot=slot, h=h):
                            dma("sp", KT[:, 0, :], K_d[(slot * 8 + h) * 96:(slot * 8 + h + 1) * 96, :], [Kall_b], [KT.b], KT.b)
                            dma("sp", VT[:].rearrange("p k c -> p (k c)"), V_d[(slot * 8 + h) * 128:(slot * 8 + h + 1) * 128, :], [Vall_b], [VT.b], VT.b)
                    else:
                        def loadf(KT=KT, VT=VT, slot=slot, h=h):
                            dma("sp", KT[0:64, :, :], K_d[(slot * 8 + h) * 128:(slot * 8 + h + 1) * 128, :].rearrange("(c d) t -> d c t", c=2), [Kall_b], [KT.b], KT.b)
                            dma("sp", VT[:].rearrange("p k c -> p (k c)"), V_d[(slot * 8 + h) * 128:(slot * 8 + h + 1) * 128, :], [Vall_b], [VT.b], VT.b)
                    dep = g >= gmin
                    for kb in range(4):
                        col = slot * 4 + kb
                        st = {"kind": "dense", "sidx": (gi * 4 + kb) * NCOMP}
                        st["load"] = loadf if kb == 0 else None
                        st["kt"] = [KT[:, c, kb * 128:(kb + 1) * 128] for c in range(NCOMP)]
                        st["qt"] = [QT[:, c, :] for c in range(NCOMP)]
                        st["rd_qk"] = [KT.b, QT.b]
                        st["v"] = VT[:, kb, :]
                        st["rd_v"] = [VT.b]
                        st["mask"] = kidx[:, col:col + 1] if dep else None
                        if stage == 3:
                            st["steep"] = steep
                            st["rd_b"] = [ball.b]
                            st["bias"] = [ball[:, qb, col:col + 1] for qb in range(4)] if steep else ball[:, 0, col:col + 1]
                        else:
                            st["bias"] = None
                        st["last"] = (gi == len(glist) - 1 and kb == 3)
                        st["h"] = h
                        if stage == 2:
                            def accf(c, qb, ACC=ACC, started=started):
                                f = not started.get("a", False)
                                started["a"] = True
                                return ACC[:, qb * 65:(qb + 1) * 65], ACC.b, f
                        else:
                            def accf(c, qb, started=started):
                                A = ACCS[qb]
                                f = not started.get(qb, False)
                                started[qb] = True
                                return A[:, c * 129:(c + 1) * 129], A.b, f
                        st["acc"] = accf
                        if stage == 2:
                            st["ACC"] = ACC
                        steps.append(st)

                def fin_cb(pst):
                    hh = pst["h"]
                    if stage == 2:
                        A = pst["ACC"]
                        a3 = A[:, 0:260].rearrange("p (q c) -> p q c", q=4)
                        R.op("dve", lambda e, a3=a3: e.reciprocal(out=rec4[:, 0:4], in_=a3[:, :, 64]), writes=[A.b, rec4.b])
                        R.op("dve", lambda e, a3=a3, hh=hh: e.tensor_tensor(out=Ot[:, :, 512 + hh * 64:512 + (hh + 1) * 64], in0=a3[:, :, 0:64], in1=bc(rec4[:, 0:4], 2, [128, 4, 64]), op=ALU.mult),
                             reads=[rec4.b], writes=[A.b, Ot.b])
                    else:
                        for qb in range(4):
                            A = ACCS[qb]
                            a3 = A[:, 0:258].rearrange("p (c d) -> p c d", c=2)
                            R.op("dve", lambda e, a3=a3: e.reciprocal(out=rr2[:, 0:2], in_=a3[:, :, 128]), writes=[A.b, rr2.b])
                            R.op("dve", lambda e: e.tensor_tensor(out=rr2[:, 1:2], in0=rr2[:, 1:2], in1=lamt[:, 3:4], op=ALU.mult), reads=[rr2.b, lamt.b], writes=[rr2.b])
                            R.op("act", lambda e, a3=a3: e.activation(out=t1[:], in_=a3[:, 0, 0:128], func=AF.Copy, scale=rr2[:, 0:1]), reads=[rr2.b], writes=[A.b, t1.b])
                            R.op("dve", lambda e, a3=a3, qb=qb: e.scalar_tensor_tensor(out=dd[:, qb, :], in0=a3[:, 1, 0:128], scalar=rr2[:, 1:2], in1=t1[:], op0=ALU.mult, op1=ALU.add),
                                 reads=[rr2.b, t1.b], writes=[A.b, dd.b])
                            R.op("act", lambda e, qb=qb: e.activation(out=junk[:, 0:128], in_=dd[:, qb, :], func=AF.Square, accum_out=ssd[:, qb:qb + 1]), reads=[dd.b], writes=[junk.b, ssd.b])
                        rstd_ops(rsd[:, 0:4], ssd[:, 0:4], 128.0, rsd.b, ssd.b, rtmp)
                        for qb in range(4):
                            R.op("dve", lambda e, qb=qb, hh=hh: e.scalar_tensor_tensor(out=Ot[:, qb, hh * 128:(hh + 1) * 128], in0=dd[:, qb, :], scalar=rsd[:, qb:qb + 1], in1=gsub[:],
                                                                                       op0=ALU.mult, op1=ALU.mult), reads=[dd.b, rsd.b, gsub.b], writes=[Ot.b])
                attn_steps(QT, steps, 512, fin_cb)

            for tb in range(4):
                tp = TP[tp_rr[0] % 2]
                tp_rr[0] += 1
                for kc in range(8):
                    R.op("pe", lambda e, kc=kc, tb=tb, tp=tp: e.transpose(out=tp[:, kc * 128:(kc + 1) * 128], in_=Ot[:, tb, kc * 128:(kc + 1) * 128], identity=ident[:]),
                         reads=[Ot.b, ident.b], writes=[tp.b])
                R.op("act", lambda e, tb=tb, tp=tp: e.copy(out=OT[:, :, tb * 128:(tb + 1) * 128], in_=tp[:, :].rearrange("p (k t) -> p k t", k=8)), writes=[tp.b, OT.b])
            k_ = 0
            for ng in range(2):
                wo_c = wstream(WO, 8, 1024, ng * 512)
                for tb in range(4):
                    ps = PS[k_ % 6]
                    k_ += 1
                    for kc in range(8):
                        R.op("pe", lambda e, kc=kc, tb=tb, ps=ps, wo_c=wo_c: e.matmul(ps[:, :], lhsT=OT[:, kc, tb * 128:(tb + 1) * 128], rhs=wo_c[:, kc, :],
                                                                                      start=(kc == 0), stop=(kc == 7)), reads=[OT.b, wo_c.b], writes=[ps.b])
                    R.op("dve", lambda e, tb=tb, ng=ng, ps=ps: e.tensor_tensor(out=xt[:, tb, ng * 512:(ng + 1) * 512], in0=xt[:, tb, ng * 512:(ng + 1) * 512], in1=ps[:, :], op=ALU.add),
                         reads=[xt.b], writes=[ps.b, xt.b])
            if dbg:
                if lt == 0:
                    dx1 = dout("dbg_x1", [TOK, D])
                    dx2 = dout("dbg_x2", [TOK, D])
                    dOt = dout("dbg_Ot", [TOK, D], BF16)
                dma("pool", dx1[lt * 512:(lt + 1) * 512, :].rearrange("(t p) c -> p t c", p=128), xt[:], [xt.b], [Buf("x")], xt.b)
                dma("pool", dOt[lt * 512:(lt + 1) * 512, :].rearrange("(t p) c -> p t c", p=128), Ot[:], [Ot.b], [Buf("x")], Ot.b)
                final.append(Ot.b)
            rms_T(xt, [0, 1, 2, 3], "mqn" + sfx)
            mq_t = wstream("mq" + sfx, 8, 512, 0)
            for h in range(4):
                ps = PS[2 + h % 2]
                for kc in range(8):
                    R.op("pe", lambda e, kc=kc, h=h, ps=ps, mq_t=mq_t: e.matmul(ps[:, :], lhsT=mq_t[:, kc, h * 128:(h + 1) * 128], rhs=hT[:, kc, 0:512], start=(kc == 0), stop=(kc == 7)),
                         reads=[mq_t.b, hT.b], writes=[ps.b])
                R.op("act", lambda e, ps=ps: e.activation(out=sqb[:], in_=ps[:, :], func=AF.Square), writes=[ps.b, sqb.b])
                ps2 = PS[4 + h % 2]
                R.op("pe", lambda e, ps2=ps2: e.matmul(ps2[:, :], lhsT=ones_bf[:, :], rhs=sqb[:, :], start=True, stop=True), reads=[ones_bf.b, sqb.b], writes=[ps2.b])
                R.op("dve", lambda e, ps2=ps2: e.tensor_scalar(out=rstf[:], in0=ps2[:, :], scalar1=1.0 / 128, scalar2=None, op0=ALU.mult), writes=[ps2.b, rstf.b])
                R.op("act", lambda e: e.activation(out=rstf[:], in_=rstf[:], func=AF.Sqrt, bias=epst[:, 0:1]), reads=[rstf.b, epst.b], writes=[rstf.b])
                R.op("dve", lambda e: e.reciprocal(out=rstf[:], in_=rstf[:]), reads=[rstf.b], writes=[rstf.b])
                R.op("dve", lambda e, h=h, ps=ps: e.scalar_tensor_tensor(out=qTm[:, h, :], in0=ps[:, :], scalar=gmq_col[:, 0:1], in1=rstf[:], op0=ALU.mult, op1=ALU.mult),
                     reads=[gmq_col.b, rstf.b], writes=[ps.b, qTm.b])
            for h in range(4):
                started = {}
                steps = []
                for mb in range(2):
                    st = {"kind": "mem", "sidx": mb, "kt": [KmemT[:, h, mb * 128:(mb + 1) * 128]], "qt": [qTm[:, h, :]], "rd_qk": [KmemT.b, qTm.b],
                          "v": Vmem[:, mb, h, :], "rd_v": [Vmem.b], "mask": None, "bias": None, "last": mb == 1, "h": h}

                    def accf(c, qb, started=started):
                        A = ACCS[qb]
                        f = not started.get(qb, False)
                        started[qb] = True
                        return A[:, 0:129], A.b, f
                    st["acc"] = accf
                    steps.append(st)

                def fin_mem(pst):
                    hh = pst["h"]
                    for qb in range(4):
                        A = ACCS[qb]
                        R.op("dve", lambda e, A=A, qb=qb: e.reciprocal(out=rec4[:, qb:qb + 1], in_=A[:, 128:129]), writes=[A.b, rec4.b])
                        R.op("act", lambda e, A=A, qb=qb, hh=hh: e.activation(out=Om[:, qb, hh * 128:(hh + 1) * 128], in_=A[:, 0:128], func=AF.Copy, scale=rec4[:, qb:qb + 1]),
                             reads=[rec4.b], writes=[A.b, Om.b])
                attn_steps(None, steps, 512, fin_mem)
            for tb in range(4):
                tp = TP[tp_rr[0] % 2]
                tp_rr[0] += 1
                for hh in range(4):
                    R.op("pe", lambda e, hh=hh, tb=tb, tp=tp: e.transpose(out=tp[:, hh * 128:(hh + 1) * 128], in_=Om[:, tb, hh * 128:(hh + 1) * 128], identity=ident[:]),
                         reads=[Om.b, ident.b], writes=[tp.b])
                R.op("act", lambda e, tb=tb, tp=tp: e.copy(out=OmT[:, :, tb * 128:(tb + 1) * 128], in_=tp[:, 0:512].rearrange("p (k t) -> p k t", k=4)), writes=[tp.b, OmT.b])
            k_ = 0
            for ng in range(2):
                mo_c = wstream("mo" + sfx, 4, 1024, ng * 512)
                for tb in range(4):
                    ps = PS[k_ % 6]
                    k_ += 1
                    for hh in range(4):
                        R.op("pe", lambda e, hh=hh, tb=tb, ps=ps, mo_c=mo_c: e.matmul(ps[:, :], lhsT=OmT[:, hh, tb * 128:(tb + 1) * 128], rhs=mo_c[:, hh, :],
                                                                                      start=(hh == 0), stop=(hh == 3)), reads=[OmT.b, mo_c.b], writes=[ps.b])
                    R.op("dve", lambda e, tb=tb, ng=ng, ps=ps: e.tensor_tensor(out=xt[:, tb, ng * 512:(ng + 1) * 512], in0=xt[:, tb, ng * 512:(ng + 1) * 512], in1=ps[:, :], op=ALU.add),
                         reads=[xt.b], writes=[ps.b, xt.b])
            if dbg:
                dma("pool", dx2[lt * 512:(lt + 1) * 512, :].rearrange("(t p) c -> p t c", p=128), xt[:], [xt.b], [Buf("x")], xt.b)
            rms_T(xt, [0, 1, 2, 3], "ffn" + sfx)
            gun = "gu" + sfx
            wdn = "wd" + sfx
            for fc in range(NFC):
                gu = gus[fc % 2]
                wload("sp", gu[:].rearrange("p a b c -> p (a b c)"), gu.b, gun, fc * 2048, 2048)
                G = PS[(2 * fc) % 6]
                U = PS[(2 * fc + 1) % 6]
                for kc in range(8):
                    R.op("pe", lambda e, kc=kc, gu=gu, G=G: e.matmul(G[:, :], lhsT=gu[:, 0, kc, :], rhs=hT[:, kc, 0:512], start=(kc == 0), stop=(kc == 7)), reads=[gu.b, hT.b], writes=[G.b])
                for kc in range(8):
                    R.op("pe", lambda e, kc=kc, gu=gu, U=U: e.matmul(U[:, :], lhsT=gu[:, 1, kc, :], rhs=hT[:, kc, 0:512], start=(kc == 0), stop=(kc == 7)), reads=[gu.b, hT.b], writes=[U.b])
                sg = sgs[fc % 2]
                R.op("act", lambda e, sg=sg, G=G: e.activation(out=sg[:], in_=G[:, :], func=AF.Silu), writes=[G.b, sg.b])
                R.op("dve", lambda e, sg=sg, U=U, fc=fc: e.tensor_tensor(out=AT[:, fc, :], in0=sg[:], in1=U[:, :], op=ALU.mult), reads=[sg.b], writes=[U.b, AT.b])
            ci = 0
            for ng in range(2):
                for fcp in range(11):
                    wd = wds[ci % 3]
                    ci += 1
                    wload("sp", wd[:].rearrange("p a b -> p (a b)"), wd.b, wdn, (ng * 11 + fcp) * 1024, 1024)
                    for fci in range(2):
                        fc = 2 * fcp + fci
                        for tb in range(4):
                            A = ACCS[tb]
                            R.op("pe", lambda e, fc=fc, fci=fci, tb=tb, A=A, wd=wd: e.matmul(A[:, :], lhsT=AT[:, fc, tb * 128:(tb + 1) * 128], rhs=wd[:, fci, :], start=(fc == 0), stop=(fc == NFC - 1)),
                                 reads=[AT.b, wd.b], writes=[A.b])
                for tb in range(4):
                    A = ACCS[tb]
                    R.op("dve", lambda e, tb=tb, ng=ng, A=A: e.tensor_tensor(out=xt[:, tb, ng * 512:(ng + 1) * 512], in0=xt[:, tb, ng * 512:(ng + 1) * 512], in1=A[:, :], op=ALU.add),
                         reads=[xt.b], writes=[A.b, xt.b])
            ob = Buf("xo")
            dma("pool", xo_d[lt * 512:(lt + 1) * 512, :].rearrange("(t p) c -> p t c", p=128), xt[:], [xt.b], [ob], xt.b)
            final.append(xt.b)
            if stage == 2:
                rms_T(xt, [0, 1, 2, 3], "mix1")
                for grp in range(6):
                    wq_c = wstream("qkv", 8, 3072, grp * 512)
                    if grp < 4:
                        stg = QKst[grp % 2]
                    else:
                        stg = Vst[grp % 2]
                    for tb in range(4):
                        ps = PS[(grp * 4 + tb) % 6]
                        proj_tm(ps, 512, tb * 128, wq_c, 0)
                        if grp < 4:
                            gain = gq_s[:, :] if grp < 2 else GR("od_kg")
                            gbuf = gq_s.b if grp < 2 else grow.b
                            R.op("act", lambda e, ps=ps: e.activation(out=sqq[:], in_=ps[:, :], func=AF.Square), writes=[ps.b, sqq.b])
                            R.op("dve", lambda e: e.tensor_reduce(out=ssq[:], in_=sqq[:, :].rearrange("p (h d) -> p h d", h=8), axis=AX.X, op=ALU.add), reads=[sqq.b], writes=[ssq.b])
                            rstd_ops(rsq[:, 0:8], ssq[:, 0:8], 64.0, rsq.b, ssq.b, rtmp)
                            R.op("dve", lambda e, ps=ps: e.tensor_tensor(out=tmpQ[:, :].rearrange("p (h d) -> p h d", h=8), in0=ps[:, :].rearrange("p (h d) -> p h d", h=8),
                                                                         in1=bc(rsq[:, 0:8], 2, [128, 8, 64]), op=ALU.mult), reads=[rsq.b], writes=[ps.b, tmpQ.b])
                            R.op("pool", lambda e, gain=gain: e.tensor_tensor(out=Qtm2[:, :].rearrange("p (h d) -> p h d", h=8), in0=tmpQ[:, :].rearrange("p (h d) -> p h d", h=8),
                                                                              in1=bc(gain, 1, [128, 8, 64]), op=ALU.mult), reads=[tmpQ.b, gbuf], writes=[Qtm2.b])
                            tp = TP[tp_rr[0] % 2]
                            tp_rr[0] += 1
                            for i8 in range(8):
                                R.op("pe", lambda e, i8=i8, tp=tp: e.transpose(out=tp[0:64, i8 * 128:(i8 + 1) * 128], in_=Qtm2[:, i8 * 64:(i8 + 1) * 64], identity=ident[:]),
                                     reads=[Qtm2.b, ident.b], writes=[tp.b])
                            R.op("act", lambda e, tp=tp, stg=stg, tb=tb: e.copy(out=stg[:, :, tb * 128:(tb + 1) * 128], in_=tp[0:64, :].rearrange("p (h t) -> p h t", h=8)),
                                 writes=[tp.b, stg.b])
                        else:
                            R.op("act", lambda e, ps=ps, stg=stg, tb=tb: e.copy(out=stg[:, :, tb, 0:128], in_=ps[:, :].rearrange("p (h d) -> p h d", h=4)), writes=[ps.b, stg.b])
                    ob2 = Buf("o2")
                    if grp < 2:
                        dma("pool", Qd_o[lt, grp * 4:(grp + 1) * 4].rearrange("h c d t -> d (h c) t"), stg[:], [stg.b], [ob2], stg.b)
                    elif grp < 4:
                        g2 = grp - 2
                        dma("pool", Kd_o[lt * 1024 + g2 * 512:lt * 1024 + (g2 + 1) * 512, :].rearrange("(hc d) t -> d hc t", d=64), stg[:], [stg.b], [ob2], stg.b)
                    else:
                        g2 = grp - 4
                        dma("pool", Vd_o[lt * 1024 + g2 * 512:lt * 1024 + (g2 + 1) * 512, :].rearrange("(h p) c -> p h c", h=4), stg[:].rearrange("p h k c -> p h (k c)"), [stg.b], [ob2], stg.b)
                    final.append(stg.b)

    R.emit(final_bufs=final)
    return nc


_DBG = None
_STOP_AFTER = 99


def _run(nc, in_maps):
    res = run_bass_kernel_spmd(nc, in_maps, core_ids=list(range(8)))
    return res.results


def kernel(**inputs):
    x = np.asarray(inputs["x"], np.float32)
    mem = np.asarray(inputs["mem"], np.float32)
    pos = np.asarray(inputs["positions"])
    B, S, _ = x.shape
    NT = S // 512
    NL = NT // 4
    TOK = NL * 512
    NB = NL * 4
    Wb, grow, gcol = host_prep(inputs)
    C = const_tables()
    core_tiles = [[gtile(c % 4, lt) for lt in range(NL)] for c in range(8)]

    def shard_rows(arr_b, tiles):
        return np.ascontiguousarray(np.concatenate([arr_b[g * 512:(g + 1) * 512] for g in tiles], axis=0))

    xs, xhs, poss, hvs, qidxs, q0s = [], [], [], [], [], []
    for c in range(8):
        b = c // 4
        tiles = core_tiles[c]
        xs.append(shard_rows(x[b], tiles))
        halo = []
        hvv = np.ones((128, NL), np.float32)
        for i, g in enumerate(tiles):
            if g == 0:
                halo.append(np.zeros((128, D), np.float32))
                hvv[:, i] = 0.0
            else:
                halo.append(x[b, g * 512 - 128:g * 512])
        xhs.append(np.ascontiguousarray(np.concatenate(halo, axis=0)))
        hvs.append(hvv)
        pl = shard_rows(pos[b], tiles).astype(np.int32)
        poss.append(np.ascontiguousarray(pl.reshape(NB, 128).T))
        qi = np.concatenate([np.arange(g * 512, (g + 1) * 512) for g in tiles]).astype(np.float32)
        qidxs.append(qi)
        q0s.append(np.ascontiguousarray(np.broadcast_to(np.asarray([g * 512 for g in tiles], np.float32)[None, :], (128, NL))))
    kidx = np.zeros((128, NT * 4), np.float32)
    for jj in range(4):
        for ltk in range(NL):
            g = gtile(jj, ltk)
            for kb in range(4):
                kidx[:, (jj * NL + ltk) * 4 + kb] = g * 512 + kb * 128 + np.arange(128)

    def common(c):
        return {"c_ident": C["ident"], "grow": grow, "gcol": gcol}

    nc1 = build(1, S, dbg=(_DBG is not None))
    in1 = []
    for c in range(8):
        m = common(c)
        m.update({"x": xs[c], "xh": xhs[c], "pos": poss[c], "hv": hvs[c], "c_mcur": C["mcur"], "c_mprev": C["mprev"],
                  "c_swa_off": C["swa_off"], "c_swa_qaug": C["swa_qaug"], "c_swa_kaug": C["swa_kaug"]})
        for n in STAGE_W[1]:
            m["W_" + n] = Wb[n]
        in1.append(m)
    r1 = _run(nc1, in1)
    if _DBG is not None:
        _DBG["r1"] = r1
        _DBG["core_tiles"] = core_tiles
    if _STOP_AFTER <= 1:
        return None
    nc2 = build(2, S, dbg=(_DBG is not None))
    in2 = []
    for c in range(8):
        b = c // 4
        grp = range(4 * b, 4 * b + 4)
        m = common(c)
        m.update({"x": xs[c], "mem": mem[b], "qidx": qidxs[c], "kidx": kidx, "q0": q0s[c],
                  "Oa": r1[c]["Oa"], "Qm": r1[c]["Qm"],
                  "Kall": np.concatenate([r1[k]["Km"] for k in grp], axis=0),
                  "Vall": np.concatenate([r1[k]["Vm"] for k in grp], axis=0)})
        for n in STAGE_W[2]:
            m["W_" + n] = Wb[n]
        in2.append(m)
    r2 = _run(nc2, in2)
    if _DBG is not None:
        _DBG["r2"] = r2
    if _STOP_AFTER <= 2:
        return None
    nc3 = build(3, S)
    in3 = []
    for c in range(8):
        b = c // 4
        grp = range(4 * b, 4 * b + 4)
        m = common(c)
        m.update({"x": r2[c]["xo"], "mem": mem[b], "qidx": qidxs[c], "kidx": kidx, "q0": q0s[c],
                  "Qd": r2[c]["Qd"], "c_d_qaug": C["d_qaug"],
                  "Kall": np.concatenate([r2[k]["Kd"] for k in grp], axis=0),
                  "Vall": np.concatenate([r2[k]["Vd"] for k in grp], axis=0)})
        for n in STAGE_W[3]:
            m["W_" + n] = Wb[n]
        in3.append(m)
    r3 = _run(nc3, in3)
    out = np.zeros((B, S, D), np.float32)
    for c in range(8):
        b = c // 4
        for i, g in enumerate(core_tiles[c]):
            out[b, g * 512:(g + 1) * 512] = r3[c]["xo"][i * 512:(i + 1) * 512]
    return out
```
